# Optimizing a Trainium2 kernel written in Bass

```python
import jax, jax.numpy as jnp
from jax import lax
import numpy as np

D_MODEL = 1024
BATCH = 8
SEQ = 4096
DEPTH = 1

DN_HEADS = 8
DN_HEAD_DIM = 64
DN_WIDTH = DN_HEADS * DN_HEAD_DIM
DN_CONV = 3
DN_CHUNK = 64
ATT_GROUPS = ((128, 1), (512, 4), (2048, 16))
N_GROUPS = 3
ATT_HEADS_PER_GROUP = 8
ATT_HEAD_DIM = 64
ATT_HEADS = N_GROUPS * ATT_HEADS_PER_GROUP
ATT_WIDTH = ATT_HEADS * ATT_HEAD_DIM
ATT_OUT_WIDTH = ATT_HEADS_PER_GROUP * ATT_HEAD_DIM
ATT_BLOCK = 64
ALIBI_MAX = 8.0
D_FF = 2816
FFN_CONV = 3
IN_PROJ_SIZES = (3 * DN_WIDTH, DN_WIDTH, 2 * DN_HEADS, 2 * DN_HEADS,
                 ATT_WIDTH, ATT_WIDTH, ATT_WIDTH, D_MODEL, D_MODEL)
D_IN_PROJ = 3 * DN_WIDTH + DN_WIDTH + 4 * DN_HEADS + 3 * ATT_WIDTH + 2 * D_MODEL
EPS = 1e-6
NEG_INF = -1e30

kernel_name = "hybrid_deltanet_dilated_attn_convffn_block"


def rms_norm(x, w):
    xf = x.astype(jnp.float32)
    y = xf * lax.rsqrt(jnp.mean(xf * xf, axis=-1, keepdims=True) + EPS)
    return (y * w.astype(jnp.float32)).astype(x.dtype)


def l2_normalize(t):
    tf = t.astype(jnp.float32)
    return tf * lax.rsqrt(jnp.sum(tf * tf, axis=-1, keepdims=True) + EPS)


def depthwise_conv_centred(x, w):
    width, c = w.shape
    return lax.conv_general_dilated(
        x, w[:, None, :].astype(x.dtype), window_strides=(1,),
        padding=[((width - 1) // 2, width // 2)],
        dimension_numbers=('NWC', 'WIO', 'NWC'), feature_group_count=c)


def alibi_slopes(n):
    return 2.0 ** (-ALIBI_MAX * jnp.arange(1, n + 1, dtype=jnp.float32) / n)


def gated_delta_rule_chunked(q, k, v, g, beta):
    f32 = jnp.float32
    n, h, t, dk = q.shape
    dv = v.shape[-1]
    c = DN_CHUNK
    nc = t // c
    q = q.astype(f32).reshape(n, h, nc, c, dk) * (dk ** -0.5)
    k = k.astype(f32).reshape(n, h, nc, c, dk)
    v = v.astype(f32).reshape(n, h, nc, c, dv)
    beta = beta.astype(f32).reshape(n, h, nc, c)
    g = jnp.cumsum(g.astype(f32).reshape(n, h, nc, c), axis=-1)
    idx = jnp.arange(c)
    incl = idx[:, None] >= idx[None, :]
    strict = idx[:, None] > idx[None, :]
    decay = jnp.exp(jnp.where(incl, g[..., :, None] - g[..., None, :], -jnp.inf))
    kb = k * beta[..., None]
    a_low = jnp.where(strict, jnp.einsum('nhcid,nhcjd->nhcij', kb, k) * decay, 0.0)
    eye = jnp.eye(c, dtype=f32)
    tmat = lax.linalg.triangular_solve(eye + a_low, jnp.broadcast_to(eye, a_low.shape),
                                       left_side=True, lower=True, unit_diagonal=True)
    u = jnp.einsum('nhcij,nhcjv->nhciv', tmat, v * beta[..., None])
    w = jnp.einsum('nhcij,nhcjk->nhcik', tmat, kb * jnp.exp(g)[..., None])
    qk = jnp.einsum('nhcid,nhcjd->nhcij', q, k) * decay
    g_last = g[..., -1]
    q_dec = q * jnp.exp(g)[..., None]
    k_dec = k * jnp.exp(g_last[..., None] - g)[..., None]
    xs = tuple(jnp.moveaxis(arr, 2, 0) for arr in (u, w, qk, q_dec, k_dec, g_last))

    def step(state, inp):
        u_i, w_i, qk_i, qd_i, kd_i, gl_i = inp
        v_new = u_i - jnp.einsum('nhck,nhkv->nhcv', w_i, state)
        o_i = jnp.einsum('nhck,nhkv->nhcv', qd_i, state) + jnp.einsum('nhij,nhjv->nhiv', qk_i, v_new)
        state = state * jnp.exp(gl_i)[..., None, None] + jnp.einsum('nhck,nhcv->nhkv', kd_i, v_new)
        return state, o_i

    _, o = lax.scan(step, jnp.zeros((n, h, dk, dv), f32), xs)
    return jnp.moveaxis(o, 0, 2).reshape(n, h, t, dv)


def dilated_window_attention(q, k, v, slopes, dilation, radius):
    b, s, h, dh = q.shape
    blk = ATT_BLOCK
    sub = s // dilation
    nb = -(-sub // blk)
    lp = nb * blk

    def to_sub(a):
        return a.reshape(b, sub, dilation, h, dh).transpose(0, 2, 3, 1, 4)

    qs = jnp.pad(to_sub(q), ((0, 0),) * 3 + ((0, lp - sub), (0, 0))).reshape(b, dilation, h, nb, blk, dh)

    def windows(a):
        ap = jnp.pad(to_sub(a), ((0, 0),) * 3 + ((blk, lp - sub + blk), (0, 0)))
        ap = ap.reshape(b, dilation, h, nb + 2, blk, dh)
        return jnp.concatenate([ap[:, :, :, 0:nb], ap[:, :, :, 1:nb + 1], ap[:, :, :, 2:nb + 2]], axis=-2)

    kw, vw = windows(k), windows(v)
    blocks = jnp.arange(nb)[:, None, None]
    qi = blocks * blk + jnp.arange(blk)[None, :, None]
    kj = (blocks - 1) * blk + jnp.arange(3 * blk)[None, None, :]
    rel = jnp.abs(qi - kj)
    valid = (rel <= radius) & (kj >= 0) & (kj < sub)
    scores = jnp.einsum('bdhnqe,bdhnke->bdhnqk', qs, kw) * (dh ** -0.5)
    scores = scores - slopes[:, None, None, None] * (rel * dilation).astype(jnp.float32)
    scores = jnp.where(valid, scores, NEG_INF)
    m = jnp.max(scores, axis=-1, keepdims=True)
    lse = m + jnp.log(jnp.sum(jnp.exp(scores - m), axis=-1, keepdims=True))
    o = jnp.einsum('bdhnqk,bdhnke->bdhnqe', jnp.exp(scores - lse), vw)
    o = o.reshape(b, dilation, h, lp, dh)[:, :, :, :sub].transpose(0, 3, 1, 2, 4).reshape(b, s, h, dh)
    lse = lse.reshape(b, dilation, h, lp)[..., :sub].transpose(0, 3, 1, 2).reshape(b, s, h)
    return o, lse


def hybrid_block(x, norm1_w, w_in, dn_conv_w, dn_a_log, dn_dt_bias, dn_out_norm_w,
                 attn_q_norm_w, attn_k_norm_w, w_dn_out, w_attn_out, w_o,
                 norm2_w, w_ffn_up, ffn_conv_w, w_ffn_down):
    f32 = jnp.float32
    b, s, _ = x.shape
    h = rms_norm(x, norm1_w)
    proj = h @ w_in
    points = np.cumsum(np.array(IN_PROJ_SIZES))[:-1].tolist()
    qkv_a, z_a, a_in, b_in, q_b, k_b, v_b, gate_a, gate_b = jnp.split(proj, points, axis=-1)

    qkv_a = jax.nn.silu(depthwise_conv_centred(qkv_a, dn_conv_w))
    q_a, k_a, v_a = [t.reshape(b, s, DN_HEADS, DN_HEAD_DIM).transpose(0, 2, 1, 3)
                     for t in jnp.split(qkv_a, 3, axis=-1)]
    q_a, k_a = l2_normalize(q_a), l2_normalize(k_a)
    a_dir = a_in.astype(f32).reshape(b, s, 2, DN_HEADS)
    beta = jax.nn.sigmoid(b_in.astype(f32).reshape(b, s, 2, DN_HEADS))
    g = -jnp.exp(dn_a_log.astype(f32)) * jax.nn.softplus(a_dir + dn_dt_bias.astype(f32))
    g = g.transpose(2, 0, 3, 1)
    beta = beta.transpose(2, 0, 3, 1)
    both = lambda t: jnp.concatenate([t, jnp.flip(t, axis=2)], axis=0)
    o2 = gated_delta_rule_chunked(both(q_a), both(k_a), both(v_a),
                                  jnp.concatenate([g[0], jnp.flip(g[1], axis=2)], axis=0),
                                  jnp.concatenate([beta[0], jnp.flip(beta[1], axis=2)], axis=0))
    o_a = (o2[:b] + jnp.flip(o2[b:], axis=2)).transpose(0, 2, 1, 3)
    z = z_a.astype(f32).reshape(b, s, DN_HEADS, DN_HEAD_DIM)
    o_a = rms_norm(o_a, dn_out_norm_w) * jax.nn.silu(z)
    y_a = o_a.reshape(b, s, DN_WIDTH).astype(x.dtype) @ w_dn_out

    heads = lambda t: t.astype(f32).reshape(b, s, N_GROUPS, ATT_HEADS_PER_GROUP, ATT_HEAD_DIM)
    q_b = rms_norm(heads(q_b), attn_q_norm_w)
    k_b = rms_norm(heads(k_b), attn_k_norm_w)
    v_b = heads(v_b)
    slopes = alibi_slopes(ATT_HEADS).reshape(N_GROUPS, ATT_HEADS_PER_GROUP)
    outs, lses = [], []
    for gi, (window, dil) in enumerate(ATT_GROUPS):
        o_g, l_g = dilated_window_attention(q_b[:, :, gi], k_b[:, :, gi], v_b[:, :, gi],
                                            slopes[gi], dil, window // (2 * dil))
        outs.append(o_g)
        lses.append(l_g)
    wts = jax.nn.softmax(jnp.stack(lses), axis=0)
    o_b = jnp.einsum('gbsh,gbshe->bshe', wts, jnp.stack(outs))
    y_b = o_b.reshape(b, s, ATT_OUT_WIDTH).astype(x.dtype) @ w_attn_out

    mixed = jax.nn.sigmoid(gate_a) * y_a + jax.nn.sigmoid(gate_b) * y_b
    x = x + mixed @ w_o

    h2 = rms_norm(x, norm2_w)
    gate, up = jnp.split(depthwise_conv_centred(h2 @ w_ffn_up, ffn_conv_w), 2, axis=-1)
    return x + (jax.nn.silu(gate) * up) @ w_ffn_down


def setup_inputs(seed: int = 0) -> dict:
    key = jax.random.key(seed)
    ks = jax.random.split(key, 17)
    f32 = jnp.float32
    normal = lambda k, shape, scale: jax.random.normal(k, shape, f32) * scale
    dt = jnp.exp(jax.random.uniform(ks[5], (DEPTH, 2, DN_HEADS), f32, np.log(1e-3), np.log(1e-1)))
    return {
        "x": normal(ks[0], (BATCH, SEQ, D_MODEL), 1.0),
        "norm1_w": 1.0 + normal(ks[1], (DEPTH, D_MODEL), 0.05),
        "w_in": normal(ks[2], (DEPTH, D_MODEL, D_IN_PROJ), D_MODEL ** -0.5),
        "dn_conv_w": normal(ks[3], (DEPTH, DN_CONV, 3 * DN_WIDTH), DN_CONV ** -0.5),
        "dn_a_log": jnp.log(jax.random.uniform(ks[4], (DEPTH, 2, DN_HEADS), f32, 1.0, 16.0)),
        "dn_dt_bias": dt + jnp.log(-jnp.expm1(-dt)),
        "dn_out_norm_w": 1.0 + normal(ks[6], (DEPTH, DN_HEAD_DIM), 0.05),
        "attn_q_norm_w": 1.0 + normal(ks[7], (DEPTH, ATT_HEAD_DIM), 0.05),
        "attn_k_norm_w": 1.0 + normal(ks[8], (DEPTH, ATT_HEAD_DIM), 0.05),
        "w_dn_out": normal(ks[9], (DEPTH, DN_WIDTH, D_MODEL), DN_WIDTH ** -0.5),
        "w_attn_out": normal(ks[10], (DEPTH, ATT_OUT_WIDTH, D_MODEL), ATT_OUT_WIDTH ** -0.5),
        "w_o": normal(ks[11], (DEPTH, D_MODEL, D_MODEL), D_MODEL ** -0.5),
        "norm2_w": 1.0 + normal(ks[12], (DEPTH, D_MODEL), 0.05),
        "w_ffn_up": normal(ks[13], (DEPTH, D_MODEL, 2 * D_FF), D_MODEL ** -0.5),
        "ffn_conv_w": normal(ks[14], (DEPTH, FFN_CONV, 2 * D_FF), FFN_CONV ** -0.5),
        "w_ffn_down": normal(ks[15], (DEPTH, D_FF, D_MODEL), D_FF ** -0.5),
    }


def reference(x, norm1_w, w_in, dn_conv_w, dn_a_log, dn_dt_bias, dn_out_norm_w,
              attn_q_norm_w, attn_k_norm_w, w_dn_out, w_attn_out, w_o,
              norm2_w, w_ffn_up, ffn_conv_w, w_ffn_down):
    for layer in range(DEPTH):
        x = hybrid_block(x, norm1_w[layer], w_in[layer], dn_conv_w[layer], dn_a_log[layer],
                         dn_dt_bias[layer], dn_out_norm_w[layer], attn_q_norm_w[layer],
                         attn_k_norm_w[layer], w_dn_out[layer], w_attn_out[layer], w_o[layer],
                         norm2_w[layer], w_ffn_up[layer], ffn_conv_w[layer], w_ffn_down[layer])
    return x
```

```python
import numpy as np
import ml_dtypes
from contextlib import ExitStack
import concourse.bass as bass
import concourse.mybir as mybir
from concourse.bass_utils import run_bass_kernel_spmd

F32 = mybir.dt.float32
BF16 = mybir.dt.bfloat16
AF = mybir.ActivationFunctionType
ALU = mybir.AluOpType
AX = mybir.AxisListType

S = 4096
D = 1024
NT = 32
KC = 8
DIN = 8736
DFF = 2816
EPS = 1e-6
NEG = -30000.0

C_QKV = 0
C_Z = 1536
C_A = 2048
C_B = 2064
C_QB = 2080
C_KB = 2080 + 1536
C_VB = 2080 + 3072
C_GA = 2080 + 4608
C_GB = C_GA + 1024

ATT_GROUPS = ((128, 1), (512, 4), (2048, 16))


class Tr:
    __slots__ = ("lw", "rd")

    def __init__(self):
        self.lw = None
        self.rd = {}


def trs(n):
    return [Tr() for _ in range(n)]


NDS = 24


PHASE_MARKS = []


class Prog:
    ENG = ("pe", "dve", "act", "pool", "sp")

    def __init__(self, nc):
        self.nc = nc
        self.q = {e: [] for e in self.ENG}
        self.cnt = {e: 0 for e in self.ENG}
        self.waited = {e: {} for e in self.ENG}
        self.ndma = 0
        self.dcnt = [0] * NDS

    def _sync(self, eng, reads, writes):
        deps = {}

        def add(k, v):
            if deps.get(k, 0) < v:
                deps[k] = v

        for r in reads:
            if r.lw is not None:
                add(*r.lw)
        for w in writes:
            if w.lw is not None:
                add(*w.lw)
            for k, v in w.rd.items():
                add(k, v)
        wt = self.waited[eng]
        for k, v in deps.items():
            if eng == "pe" and k == "pe":
                continue
            if wt.get(k, 0) < v:
                self.q[eng].append(("w", k, v))
                wt[k] = v

    def op(self, eng, fn, reads=(), writes=()):
        self._sync(eng, reads, writes)
        self.cnt[eng] += 1
        seq = self.cnt[eng]
        self.q[eng].append(("o", fn, eng, 1))
        for r in reads:
            if r.rd.get(eng, 0) < seq:
                r.rd[eng] = seq
        for w in writes:
            w.lw = (eng, seq)
            w.rd = {}

    def dma(self, fn, reads=(), writes=(), eng="sp"):
        self._sync(eng, reads, writes)
        k = self.ndma % NDS
        self.ndma += 1
        key = "d%d" % k
        prev = self.dcnt[k]
        wt = self.waited[eng]
        if prev > 0 and wt.get(key, 0) < prev:
            self.q[eng].append(("w", key, prev))
            wt[key] = prev
        self.dcnt[k] += 16
        val = self.dcnt[k]
        self.q[eng].append(("o", fn, key, 16))
        for r in reads:
            if r.rd.get(key, 0) < val:
                r.rd[key] = val
        for w in writes:
            w.lw = (key, val)
            w.rd = {}

    def finish(self):
        wt = self.waited["sp"]
        for k in range(NDS):
            key = "d%d" % k
            if self.dcnt[k] > 0 and wt.get(key, 0) < self.dcnt[k]:
                self.q["sp"].append(("w", key, self.dcnt[k]))
        for e in ("pe", "dve", "act", "pool"):
            if self.cnt[e] > 0:
                self.q["sp"].append(("w", e, self.cnt[e]))

    def emit(self):
        nc = self.nc
        with ExitStack() as st:
            sems = {}
            for e in ("pe", "dve", "act", "pool"):
                sems[e] = st.enter_context(nc.semaphore("s_" + e))
            for k in range(NDS):
                sems["d%d" % k] = st.enter_context(nc.semaphore("s_d%d" % k))
            block = st.enter_context(nc.Block())

            def run(e, name):
                for it in self.q[name]:
                    if it[0] == "w":
                        e.wait_ge(sems[it[1]], it[2])
                    else:
                        ins = it[1](e)
                        ins.then_inc(sems[it[2]], it[3])

            @block.tensor
            def _(e):
                run(e, "pe")

            @block.vector
            def _(e):
                run(e, "dve")

            @block.scalar
            def _(e):
                run(e, "act")

            @block.gpsimd
            def _(e):
                run(e, "pool")

            @block.sync
            def _(e):
                run(e, "sp")


def _consts():
    t = np.arange(128)
    c = {}
    c["ident"] = np.eye(128, dtype=np.float32)
    c["ones"] = np.ones((128, 128), np.float32)
    bo = np.zeros((128, 128), np.float32)
    bo[:64, :64] = 1.0 / 64
    bo[64:, 64:] = 1.0 / 64
    c["bones"] = bo
    bo1 = np.zeros((128, 128), np.float32)
    bo1[:64, :64] = 1.0
    bo1[64:, 64:] = 1.0
    c["bones1"] = bo1
    T_, I_ = t[:, None], t[None, :]
    L = np.stack([(T_ <= I_), (T_ >= I_)]).astype(np.float32)
    Sm = np.stack([(T_ > I_), (T_ < I_)]).astype(np.float32)
    negS = np.stack([np.where(T_ > I_, 0.0, NEG), np.where(T_ < I_, 0.0, NEG)]).astype(np.float32)
    negST = np.stack([negS[0].T, negS[1].T])
    negI = np.stack([np.where(T_ >= I_, 0.0, NEG), np.where(T_ <= I_, 0.0, NEG)]).astype(np.float32)
    negIT = np.stack([negI[0].T, negI[1].T])
    c["L"] = L
    c["Sm"] = Sm
    c["negS"] = negS
    blk = (T_ // 64 == I_ // 64)
    negST = np.stack([np.where(blk, negST[d], NEG) for d in range(2)])
    c["negST2"] = np.stack([np.stack([negST[d], negST[d]]) for d in range(2)])
    c["off"] = np.stack([((T_ >= 64) & (I_ < 64)), ((T_ < 64) & (I_ >= 64))]).astype(np.float32)
    c["negIT2"] = np.stack([np.stack([negIT[d], negIT[d]]) for d in range(2)])
    return c


def _pack_bf16_consts():
    c = _consts()
    parts = [c["ident"], c["ones"], c["bones"], c["bones1"],
             c["L"][0], c["L"][1], c["Sm"][0], c["Sm"][1],
             c["negS"][0], c["negS"][1],
             c["negST2"][0, 0], c["negST2"][0, 1], c["negST2"][1, 0], c["negST2"][1, 1],
             c["negIT2"][0, 0], c["negIT2"][0, 1], c["negIT2"][1, 0], c["negIT2"][1, 1],
             c["off"][0], c["off"][1]]
    arr = np.concatenate(parts, axis=1)
    return arr.astype(ml_dtypes.bfloat16)


CB_IDENT, CB_ONES, CB_BONES, CB_BONES1, CB_L, CB_SM, CB_NEGS, CB_NEGST, CB_NEGIT = 0, 1, 2, 3, 4, 6, 8, 10, 14
CB_OFF = 18
NCB = 20


def _attn_masks():
    slopes = 2.0 ** (-8.0 * np.arange(1, 25, dtype=np.float32) / 24).reshape(3, 8)
    t = np.arange(128)
    J, I = t[:, None], t[None, :]
    out = np.zeros((3, 8, 2, 128, 128), np.float32)
    for g, (_, dil) in enumerate(ATT_GROUPS):
        for h in range(8):
            for ab, sh in enumerate((64, -64)):
                rel = np.abs(I - J + sh).astype(np.float32)
                out[g, h, ab] = np.where(rel <= 64, np.exp(-slopes[g, h] * rel * dil), 0.0)
    return np.ascontiguousarray(out.transpose(0, 3, 1, 2, 4)).reshape(3, 128, 8 * 2 * 128)


def build(debug=None):
    nc = bass.Bass("TRN2", target_bir_lowering=False)
    P = Prog(nc)
    dram = {}

    def din(name, shape, dt=F32):
        dram[name] = nc.dram_tensor(name, list(shape), dt, kind="ExternalInput").ap()
        return dram[name]

    x_d = din("x", [S, D])
    w_in = din("w_in", [D, DIN])
    w_ab = din("w_ab", [D, 32])
    n1w = din("norm1_w", [128, KC])
    n2w = din("norm2_w", [128, KC])
    dn_cw = din("dn_conv_w", [128, 12, 3])
    dn_alog = din("dn_a_log", [1, 16])
    dn_dtb = din("dn_dt_bias", [1, 16])
    dn_onw = din("dn_out_norm_w", [1, 64])
    aq_w = din("attn_q_norm_w", [128, 1])
    ak_w = din("attn_k_norm_w", [128, 1])
    w_dno = din("w_dn_out", [512, D])
    w_ato = din("w_attn_out", [512, D])
    w_o = din("w_o", [D, D])
    w_up = din("w_ffn_up", [D, 2 * DFF])
    f_cw = din("ffn_conv_w", [128, 44, 3])
    w_dn = din("w_ffn_down", [DFF, D])
    cb_d = din("cb", [128, NCB * 128], BF16)
    am_d = din("amask", [3, 128, 8 * 2 * 128])
    out_d = nc.dram_tensor("out", [S, D], F32, kind="ExternalOutput").ap()
    oaT_d = nc.dram_tensor("oaT_scr", [512, S], BF16, kind="Internal").ap()
    oB_d = [nc.dram_tensor("oB_scr%d" % g, [S, 8 * 65], BF16, kind="Internal").ap() for g in range(3)]
    oaT_tr = Tr()
    x1_d = nc.dram_tensor("x1_scr", [S, D], F32, kind="Internal").ap()
    dbg_d = None
    if debug is not None:
        dbg_d = nc.dram_tensor("dbg", list(debug["shape"]), debug.get("dt", F32), kind="ExternalOutput").ap()

    ARENA_BYTES = 206 * 1024
    arena = nc.alloc_sbuf_tensor("arena", [128, ARENA_BYTES // 4], F32)

    class Alloc:
        def __init__(self, base=0, dbg=False):
            self.off = base
            self.dbg = dbg

        def get(self, free_shape, dt, parts=128):
            n = int(np.prod(free_shape))
            nbytes = n * (2 if dt == BF16 else 4)
            nbytes = (nbytes + 31) // 32 * 32
            a = arena[0:parts, self.off // 4:(self.off + nbytes) // 4]
            self.off += nbytes
            assert self.off <= ARENA_BYTES - (0 if (debug is None or self.dbg) else 4096), ("SBUF arena overflow", self.off)
            if dt == BF16:
                a = a.bitcast(BF16)
            a = a[:, 0:n]
            if len(free_shape) == 2:
                a = a.rearrange("p (a b) -> p a b", b=free_shape[1])
            elif len(free_shape) == 3:
                a = a.rearrange("p (a b c) -> p a b c", b=free_shape[1], c=free_shape[2])
            elif len(free_shape) == 4:
                a = a.rearrange("p (a b c d) -> p a b c d", b=free_shape[1], c=free_shape[2], d=free_shape[3])
            return a

    pers = Alloc(0)
    hT = pers.get([KC, S + 2], BF16)
    hT_tr = trs(NT)
    cb = pers.get([NCB, 128], BF16)
    cb_tr = Tr()
    vec = pers.get([64], F32)
    vec_tr = Tr()
    PERS_END = pers.off

    banks = [nc.alloc_psum_tensor("bank%d" % i, [128, 512], F32) for i in range(8)]
    bank_tr = trs(8)
    bank_rr = [0]

    def bank():
        for _ in range(8):
            i = bank_rr[0] % 8
            bank_rr[0] += 1
            t_ = bank_tr[i]
            if t_.lw is None or len(t_.rd) > 0:
                return banks[i], t_
        raise RuntimeError("no free PSUM bank")

    ident = cb[:, CB_IDENT, :]
    ones_b = cb[:, CB_ONES, :]

    def mark(tag):
        PHASE_MARKS.append((tag, dict(P.cnt)))

    def barrier_all(tag=None):
        if tag is not None:
            PHASE_MARKS.append((tag, dict(P.cnt)))
        for e in P.ENG:
            wt = P.waited[e]
            for o in ("pe", "dve", "act", "pool"):
                if o == e and e == "pe":
                    continue
                if P.cnt[o] > 0 and wt.get(o, 0) < P.cnt[o]:
                    P.q[e].append(("w", o, P.cnt[o]))
                    wt[o] = P.cnt[o]
            for k in range(NDS):
                key = "d%d" % k
                if P.dcnt[k] > 0 and wt.get(key, 0) < P.dcnt[k]:
                    P.q[e].append(("w", key, P.dcnt[k]))
                    wt[key] = P.dcnt[k]

    P.dma(lambda e: e.dma_start(out=cb.rearrange("p a b -> p (a b)"), in_=cb_d), writes=[cb_tr])
    P.dma(lambda e: e.dma_start(out=vec[:, 0:8], in_=n1w), writes=[vec_tr])
    P.dma(lambda e: e.dma_start(out=vec[:, 8:16], in_=n2w), writes=[vec_tr])
    P.dma(lambda e: e.dma_start(out=vec[:, 16:17], in_=aq_w), writes=[vec_tr])
    P.dma(lambda e: e.dma_start(out=vec[:, 17:18], in_=ak_w), writes=[vec_tr])

    def load_w(dst, dst_tr, src, nkc, ncols, stage, stage_tr, cast_eng="pool"):
        for kc in range(nkc):
            sb, sb_tr = stage[kc % len(stage)], stage_tr[kc % len(stage)]
            P.dma(lambda e, sb=sb, kc=kc: e.dma_start(out=sb[:, 0:ncols], in_=src[kc * 128:(kc + 1) * 128, :]),
                  writes=[sb_tr])
            P.op(cast_eng, lambda e, sb=sb, kc=kc: e.tensor_copy(out=dst[:, kc, 0:ncols], in_=sb[:, 0:ncols]),
                 reads=[sb_tr], writes=[dst_tr])

    def rmsnorm_to_hT(get_x, wcol0, ph):
        pass

    ph = Alloc(PERS_END)
    xs = [ph.get([D], F32) for _ in range(4)]
    xs_tr = trs(4)
    sq = ph.get([D], F32)
    sq_tr = Tr()
    st = [ph.get([8], F32) for _ in range(2)]
    st_tr = trs(2)
    xn = [ph.get([D], BF16) for _ in range(2)]
    xn_tr = trs(2)

    P.op("pool", lambda e: e.memset(hT[:, :, 0:1], 0.0), writes=[hT_tr[0]])
    P.op("pool", lambda e: e.memset(hT[:, :, S + 1:S + 2], 0.0), writes=[hT_tr[NT - 1]])

    def norm_tile(xa, xa_tr, t, i2, wcol0, scr, defer=False):
        sq, sq_tr, st, st_tr, xn, xn_tr = scr
        s_, s_tr = st[i2], st_tr[i2]
        P.op("act", lambda e: e.activation(out=sq, in_=xa, func=AF.Square), reads=[xa_tr], writes=[sq_tr])
        P.op("dve", lambda e: e.tensor_reduce(out=s_[:, 0:1], in_=sq, axis=AX.X, op=ALU.add),
             reads=[sq_tr], writes=[s_tr])
        P.op("act", lambda e: e.activation(out=s_[:, 1:2], in_=s_[:, 0:1], func=AF.Sqrt, bias=EPS, scale=1.0 / D),
             reads=[s_tr], writes=[s_tr])
        P.op("dve", lambda e: e.reciprocal(out=s_[:, 2:3], in_=s_[:, 1:2]), reads=[s_tr], writes=[s_tr])
        P.op("dve", lambda e: e.tensor_scalar(out=xn[i2], in0=xa, scalar1=s_[:, 2:3], scalar2=None, op0=ALU.mult),
             reads=[xa_tr, s_tr], writes=[xn_tr[i2]])
        def part_b():
            for half in range(2):
                bk, bk_tr = bank()
                pb = bk[:, 0:256].bitcast(BF16).rearrange("p (a b) -> p a b", b=128)
                for c in range(4):
                    kc = half * 4 + c
                    P.op("pe", lambda e, c=c, kc=kc, pb=pb: e.transpose(out=pb[:, c, :], in_=xn[i2][:, kc * 128:(kc + 1) * 128],
                                                                       identity=ident),
                         reads=[xn_tr[i2], cb_tr], writes=[bk_tr])
                for c in range(4):
                    kc = half * 4 + c
                    eng = "act" if c % 2 == 0 else "dve"
                    if eng == "act":
                        P.op("act", lambda e, c=c, kc=kc, pb=pb: e.activation(
                            out=hT[:, kc, 1 + t * 128:1 + (t + 1) * 128], in_=pb[:, c, :], func=AF.Copy,
                            scale=vec[:, wcol0 + kc:wcol0 + kc + 1]), reads=[bk_tr, vec_tr], writes=[hT_tr[t]])
                    else:
                        P.op("dve", lambda e, c=c, kc=kc, pb=pb: e.tensor_scalar(
                            out=hT[:, kc, 1 + t * 128:1 + (t + 1) * 128], in0=pb[:, c, :],
                            scalar1=vec[:, wcol0 + kc:wcol0 + kc + 1], scalar2=None, op0=ALU.mult),
                            reads=[bk_tr, vec_tr], writes=[hT_tr[t]])

        if defer:
            return part_b
        part_b()

    pend0 = [None]
    for t in range(NT):
        i2 = t % 2
        i4 = t % 4
        P.dma(lambda e, t=t, i4=i4: e.dma_start(out=xs[i4], in_=x_d[t * 128:(t + 1) * 128, :]), writes=[xs_tr[i4]],
              eng=("sp" if t % 2 == 0 else "pool"))
        nb0 = norm_tile(xs[i4], xs_tr[i4], t, i2, 0, (sq, sq_tr, st, st_tr, xn, xn_tr), defer=True)
        if pend0[0] is not None:
            pend0[0]()
        pend0[0] = nb0

    pend0[0]()
    if debug is not None and debug["what"] == "hT":
        barrier_all()
        stg = ph.get([S], F32)
        stg_tr = Tr()
        for kc in range(KC):
            P.op("dve", lambda e, kc=kc: e.tensor_copy(out=stg, in_=hT[:, kc, 1:S + 1]), reads=hT_tr, writes=[stg_tr])
            P.dma(lambda e, kc=kc: e.dma_start(out=dbg_d[kc * 128:(kc + 1) * 128, :], in_=stg), reads=[stg_tr])
        P.finish()
        P.emit()
        return nc


    def mm(out, lhsT, rhs, start, stop, reads, writes):
        P.op("pe", lambda e: e.matmul(out, lhsT=lhsT, rhs=rhs, start=start, stop=stop), reads, writes)

    def tp(out, in_, reads, writes, idn=None):
        idn = ident if idn is None else idn
        P.op("pe", lambda e: e.transpose(out=out, in_=in_, identity=idn), list(reads) + [cb_tr], writes)

    def act(out, in_, func, reads, writes, bias=None, scale=None):
        kw = {}
        if bias is not None:
            kw["bias"] = bias
        if scale is not None:
            kw["scale"] = scale
        P.op("act", lambda e: e.activation(out=out, in_=in_, func=func, **kw), reads, writes)

    def tt(eng, out, in0, in1, op, reads, writes):
        P.op(eng, lambda e: e.tensor_tensor(out=out, in0=in0, in1=in1, op=op), reads, writes)

    def ts(eng, out, in0, s1, op0, reads, writes, s2=None, op1=None):
        if op1 is None:
            P.op(eng, lambda e: e.tensor_scalar(out=out, in0=in0, scalar1=s1, scalar2=None, op0=op0), reads, writes)
        else:
            P.op(eng, lambda e: e.tensor_scalar(out=out, in0=in0, scalar1=s1, scalar2=s2, op0=op0, op1=op1), reads, writes)

    def stt(out, in0, scalar, in1, op0, op1, reads, writes):
        P.op("dve", lambda e: e.scalar_tensor_tensor(out=out, in0=in0, scalar=scalar, in1=in1, op0=op0, op1=op1),
             reads, writes)

    def cp(eng, out, in_, reads, writes):
        if eng == "act":
            P.op("act", lambda e: e.activation(out=out, in_=in_, func=AF.Copy), reads, writes)
        else:
            P.op(eng, lambda e: e.tensor_copy(out=out, in_=in_), reads, writes)

    def memset(eng, ap, val, writes):
        P.op(eng, lambda e: e.memset(ap, val), (), writes)

    def dma(out, in_, reads=(), writes=(), eng="sp"):
        P.dma(lambda e: e.dma_start(out=out, in_=in_), reads, writes, eng=eng)

    def load_w2(dst, dst_tr, src, nkc, ncols, stage, stage_tr, cast_eng="pool"):
        sv = stage[:, 0:nkc * ncols].rearrange("p (k c) -> p k c", c=ncols)
        dma(sv, src.rearrange("(k p) c -> p k c", p=128), writes=[stage_tr])
        cp(cast_eng, dst, sv, [stage_tr], [dst_tr])

    def dump(ap_list, rows_each):
        da = Alloc(ARENA_BYTES - 4096, dbg=True)
        stgs = [da.get([512], F32), da.get([512], F32)]
        stg_trs = [Tr(), Tr()]
        r0 = 0
        i = 0
        for ap, tr_ in ap_list:
            pp, n = ap.shape[0], ap.shape[1]
            for c in range(0, n, 512):
                w = min(512, n - c)
                cp("dve", stgs[i % 2][0:pp, 0:w], ap[:, c:c + w], [tr_], [stg_trs[i % 2]])
                dma(dbg_d[r0:r0 + pp, c:c + w], stgs[i % 2][0:pp, 0:w], reads=[stg_trs[i % 2]])
                i += 1
            r0 += rows_each
        P.finish()
        P.emit()
        return nc

    barrier_all('p0_end')
    pa = Alloc(PERS_END)
    wstage = pa.get([KC * 128], F32)
    wstage_tr = Tr()
    prm = pa.get([64], F32)
    prm_tr = Tr()
    onw = pa.get([64], F32)
    onw_tr = Tr()
    g32 = pa.get([NT, 16], F32); g32_tr = Tr()
    gbf = pa.get([NT, 16], BF16); gbf_tr = Tr()
    lnb = pa.get([NT, 16], F32); lnb_tr = Tr()
    beta = pa.get([NT, 16], F32); beta_tr = Tr()
    sKbg = pa.get([NT, 16], F32); sKbg_tr = Tr()
    sKd = pa.get([NT, 16], F32); sKd_tr = Tr()
    sQg = pa.get([NT, 16], F32); sQg_tr = Tr()
    dec = pa.get([NT, 16], F32); dec_tr = Tr()

    A1_MARK = pa.off
    wab = pa.get([KC, 32], BF16)
    wab_tr = Tr()
    ab = pa.get([NT, 32], F32)
    ab_tr = Tr()
    gcs = pa.get([NT, 16], F32); gcs_tr = Tr()
    tmpA = pa.get([NT, 16], F32); tmpA_tr = Tr()
    dma(prm[:, 0:16], dn_alog.partition_broadcast(128), writes=[prm_tr])
    dma(prm[:, 16:32], dn_dtb.partition_broadcast(128), writes=[prm_tr])
    dma(onw[:, 0:64], dn_onw.partition_broadcast(128), writes=[onw_tr])
    dma(wstage[:, 0:KC * 32].rearrange("p (k c) -> p k c", c=32), w_ab.rearrange("(k p) c -> p k c", p=128),
        writes=[wstage_tr])
    cp("pool", wab, wstage[:, 0:KC * 32].rearrange("p (k c) -> p k c", c=32), [wstage_tr], [wab_tr])

    bkA, bkA_tr = bank()
    bkB, bkB_tr = bank()
    for n in range(NT):
        bk, bk_tr = (bkA, bkA_tr) if n < 16 else (bkB, bkB_tr)
        for d in range(2):
            tl = n if d == 0 else NT - 1 - n
            o = bk[:, (n % 16) * 32 + d * 16:(n % 16) * 32 + d * 16 + 16]
            for kc in range(KC):
                mm(o, hT[:, kc, 1 + tl * 128:1 + (tl + 1) * 128], wab[:, kc, d * 16:(d + 1) * 16],
                   kc == 0, kc == KC - 1, [hT_tr[tl], wab_tr], [bk_tr])
    cp("act", ab[:, 0:16, :], bkA[:, :].rearrange("p (a b) -> p a b", b=32), [bkA_tr], [ab_tr])
    cp("act", ab[:, 16:32, :], bkB[:, :].rearrange("p (a b) -> p a b", b=32), [bkB_tr], [ab_tr])
    ab4 = ab.rearrange("p n (d k h) -> p n d k h", d=2, k=2)
    a_v = ab4[:, :, :, 0, :]
    b_v = ab4[:, :, :, 1, :]
    g4 = g32.rearrange("p n (d h) -> p n d h", d=2)
    t4 = tmpA.rearrange("p n (d h) -> p n d h", d=2)
    l4 = lnb.rearrange("p n (d h) -> p n d h", d=2)
    act(prm[:, 32:48], prm[:, 0:16], AF.Exp, [prm_tr], [prm_tr])
    ts("dve", prm[:, 32:48], prm[:, 32:48], -1.0, ALU.mult, [prm_tr], [prm_tr])
    tt("dve", t4, a_v, prm[:, 16:32].rearrange("p (d h) -> p d h", d=2).unsqueeze(1).broadcast_to([128, NT, 2, 8]),
       ALU.add, [ab_tr, prm_tr], [tmpA_tr])
    act(tmpA, tmpA, AF.Exp, [tmpA_tr], [tmpA_tr])
    act(tmpA, tmpA, AF.Ln, [tmpA_tr], [tmpA_tr], bias=1.0)
    tt("dve", g32, tmpA, prm[:, 32:48].unsqueeze(1).broadcast_to([128, NT, 16]), ALU.mult, [tmpA_tr, prm_tr], [g32_tr])
    cp("dve", gbf, g32, [g32_tr], [gbf_tr])
    act(l4, b_v, AF.Exp, [ab_tr], [lnb_tr], scale=-1.0)
    act(lnb, lnb, AF.Ln, [lnb_tr], [lnb_tr], bias=1.0)
    act(beta, lnb, AF.Exp, [lnb_tr], [beta_tr], scale=-1.0)
    ts("dve", lnb, lnb, -1.0, ALU.mult, [lnb_tr, beta_tr], [lnb_tr])
    bkC, bkC_tr = bank()
    bkG, bkG_tr = bank()
    gb4 = gbf.rearrange("p n (d h) -> p n d h", d=2)
    c4 = bkC[:, :].rearrange("p (n d h) -> p n d h", d=2, h=8)
    for d in range(2):
        mm(c4[:, :, d, :], cb[:, CB_L + d, :], gb4[:, :, d, :], True, True, [gbf_tr, cb_tr], [bkC_tr])
    mm(bkG[:, :], ones_b, gbf.rearrange("p n c -> p (n c)"), True, True, [gbf_tr, cb_tr], [bkG_tr])
    gcf = gcs.rearrange("p n c -> p (n c)")
    cp("act", gcf, bkC[:, :], [bkC_tr], [gcs_tr])
    act(sQg.rearrange("p n c -> p (n c)"), gcf, AF.Exp, [gcs_tr], [sQg_tr], bias=float(np.log(0.125)))
    act(tmpA.rearrange("p n c -> p (n c)"), gcf, AF.Exp, [gcs_tr], [tmpA_tr])
    tt("dve", sKbg, tmpA, beta, ALU.mult, [tmpA_tr, beta_tr], [sKbg_tr])
    tt("dve", sKd.rearrange("p n c -> p (n c)"), bkG[:, :], gcf, ALU.subtract, [bkG_tr, gcs_tr], [sKd_tr])
    act(sKd, sKd, AF.Exp, [sKd_tr], [sKd_tr])
    act(dec[0:64].rearrange("p n c -> p (n c)"), bkG[0:64, :], AF.Exp, [bkG_tr], [dec_tr])

    if debug is not None and debug["what"] == "scal":
        return dump([(g32.rearrange("p n c -> p (n c)"), g32_tr), (beta.rearrange("p n c -> p (n c)"), beta_tr),
                     (sKbg.rearrange("p n c -> p (n c)"), sKbg_tr), (sKd.rearrange("p n c -> p (n c)"), sKd_tr),
                     (sQg.rearrange("p n c -> p (n c)"), sQg_tr), (dec.rearrange("p n c -> p (n c)"), dec_tr)], 128)

    barrier_all('A1_end')
    pa.off = A1_MARK
    wq = pa.get([KC, 4, 128], BF16); wq_tr = Tr()
    xT = [pa.get([S], BF16) for _ in range(3)]
    xT_tr = [Tr() for _ in range(3)]
    cwt = pa.get([12, 3], F32); cwt_tr = Tr()
    dma(cwt, dn_cw, writes=[cwt_tr])
    Vn = pa.get([4, 64], BF16); Vn_tr = Tr()
    S32 = pa.get([4, 64], F32); S32_tr = Tr()
    Sd = pa.get([4, 64], F32); Sd_tr = Tr()
    Sbf = pa.get([4, 64], BF16); Sbf_tr = Tr()
    RA = pa.off
    pj = Alloc(RA)
    pre = pj.get([S + 2], F32); pre_tr = Tr()
    cv = pj.get([S], F32); cv_tr = Tr()
    sqb = pj.get([S], BF16); sqb_tr = Tr()
    rn = pj.get([512], F32); rn_tr = Tr()
    sc = Alloc(RA)
    Oacc = sc.get([NT, 128], F32); Oacc_tr = Tr()

    def mk_set():
        B = {}
        B["Kbg"] = sc.get([4, 64], BF16); B["Kd_"] = sc.get([4, 64], BF16); B["Qg"] = sc.get([4, 64], BF16); B["Vb"] = sc.get([4, 64], BF16)
        B["tok_tr"] = [Tr() for _ in range(4)]
        B["QgT"] = sc.get([4, 128], BF16); B["QgT_tr"] = Tr()
        for nm_ in ("b1", "b2", "b3", "b4", "b5", "e0", "e1", "e2", "p0", "x0", "x1", "bo"):
            B[nm_] = sc.get([4, 128], BF16); B[nm_ + "_tr"] = Tr()
        return B

    sets = [mk_set() for _ in range(4)]
    a4 = Alloc(RA + NT * 128 * 4)
    szb = a4.get([4, 128], F32); szb_tr = Tr()
    ob = a4.get([4, 128], F32); ob_tr = Tr()
    osq = a4.get([4, 128], F32); osq_tr = Tr()
    ost = a4.get([16], F32); ost_tr = Tr()
    ogb = a4.get([4, 128], BF16); ogb_tr = Tr()
    oaT_stage = a4.get([S], BF16); oaT_stage_tr = Tr()
    pa.off = max(pj.off, sc.off, a4.off)

    identb4 = ident.unsqueeze(1).broadcast_to([128, 4, 128])
    f2 = lambda a: a.rearrange("p u c -> p (u c)")

    for hp in range(0 if (debug is not None and debug.get('skipA')) else 4):
        for j, col in enumerate((512 + hp * 128, hp * 128, 1024 + hp * 128, C_Z + hp * 128)):
            sv = wstage[:, 0:KC * 128].rearrange("p (k c) -> p k c", c=128)
            dma(sv, w_in[:, col:col + 128].rearrange("(k p) c -> p k c", p=128), writes=[wstage_tr])
            cp("act", wq[:, :, j, :], sv, [wstage_tr], [wq_tr])
        memset("pool", pre[:, 0:1], 0.0, [pre_tr])
        memset("pool", pre[:, S + 1:S + 2], 0.0, [pre_tr])
        for j in range(3):
            blk = (1, 0, 2)[j] * 4 + hp
            for tb in range(8):
                bk, bk_tr = bank()
                for kc in range(KC):
                    mm(bk[:, :], wq[:, kc, j, :], hT[:, kc, 1 + tb * 512:1 + (tb + 1) * 512], kc == 0, kc == KC - 1,
                       [wq_tr] + hT_tr[tb * 4:tb * 4 + 4], [bk_tr])
                cp("act", pre[:, 1 + tb * 512:1 + (tb + 1) * 512], bk[:, :], [bk_tr], [pre_tr])
            act(cv, pre[:, 1:S + 1], AF.Copy, [pre_tr, cwt_tr], [cv_tr], scale=cwt[:, blk, 1:2])
            stt(cv, pre[:, 0:S], cwt[:, blk, 0:1], cv, ALU.mult, ALU.add, [pre_tr, cwt_tr, cv_tr], [cv_tr])
            stt(cv, pre[:, 2:S + 2], cwt[:, blk, 2:3], cv, ALU.mult, ALU.add, [pre_tr, cwt_tr, cv_tr], [cv_tr])
            act(cv, cv, AF.Silu, [cv_tr], [cv_tr])
            if j < 2:
                act(sqb, cv, AF.Square, [cv_tr], [sqb_tr])
                for tb in range(8):
                    bk, bk_tr = bank()
                    mm(bk[:, :], cb[:, CB_BONES1, :], sqb[:, tb * 512:(tb + 1) * 512], True, True, [sqb_tr, cb_tr], [bk_tr])
                    act(rn, bk[:, :], AF.Ln, [bk_tr], [rn_tr], bias=EPS)
                    act(rn, rn, AF.Exp, [rn_tr], [rn_tr], scale=-0.5)
                    tt("dve", xT[j][:, tb * 512:(tb + 1) * 512], cv[:, tb * 512:(tb + 1) * 512], rn, ALU.mult,
                       [cv_tr, rn_tr], [xT_tr[j]])
            else:
                cp("dve", xT[2], cv, [cv_tr], [xT_tr[2]])

        if debug is not None and debug["what"] == "qkvT" and hp == debug.get("hp", 0):
            return dump([(xT[0], xT_tr[0]), (xT[1], xT_tr[1]), (xT[2], xT_tr[2])], 128)

        barrier_all('A2_end')
        memset("pool", S32, 0.0, [S32_tr])
        memset("pool", Sbf, 0.0, [Sbf_tr])

        def step_setup(n, B):
            tl = (n, NT - 1 - n)
            c0 = (tl[0] * 128, tl[1] * 128)
            hs = slice(hp * 2, hp * 2 + 2)
            tok_tr = B["tok_tr"]
            Kbg, Kd_, Qg, Vb = B["Kbg"], B["Kd_"], B["Qg"], B["Vb"]

            def scal(arr):
                return arr.rearrange("p n (d h) -> p n d h", d=2)[:, n, :, hs]

            f2 = lambda a: a.rearrange("p u c -> p (u c)")
            v4 = lambda a: a.rearrange("p (d h) c -> p d h c", d=2)
            u3 = lambda bk_: bk_[:, :].rearrange("p (u c) -> p u c", c=128)

            def m2(idx):
                return cb[:, idx:idx + 2, :].unsqueeze(2).broadcast_to([128, 2, 2, 128])

            gsc = scal(g32).unsqueeze(3).broadcast_to([128, 2, 2, 128])
            lsc = scal(lnb).unsqueeze(3).broadcast_to([128, 2, 2, 128])
            b1, b1t = B["b1"], B["b1_tr"]
            b2, b2t = B["b2"], B["b2_tr"]
            b3, b3t = B["b3"], B["b3_tr"]
            b4, b4t = B["b4"], B["b4_tr"]
            b5, b5t = B["b5"], B["b5_tr"]
            e0, e0t = B["e0"], B["e0_tr"]
            e1, e1t = B["e1"], B["e1_tr"]
            e2, e2t = B["e2"], B["e2_tr"]
            p0, p0t = B["p0"], B["p0_tr"]
            x0, x0t = B["x0"], B["x0_tr"]
            x1_, x1t = B["x1"], B["x1_tr"]
            bo, bot = B["bo"], B["bo_tr"]
            QgT, QgT_tr = B["QgT"], B["QgT_tr"]
            tt("pool", v4(b1), gsc, m2(CB_SM), ALU.mult, [g32_tr, cb_tr], [b1t])
            tt("pool", v4(b2), gsc, m2(CB_L), ALU.mult, [g32_tr, cb_tr], [b2t])
            tt("pool", v4(b3), lsc, m2(CB_NEGS), ALU.add, [lnb_tr, cb_tr], [b3t])
            tt("pool", v4(b4), lsc, ident.unsqueeze(1).unsqueeze(1).broadcast_to([128, 2, 2, 128]), ALU.mult,
               [lnb_tr, cb_tr], [b4t])
            hm = cb[:, CB_BONES1, 0:128:64]
            for d in range(2):
                tt("pool", b5[:, d * 2:d * 2 + 2, :], xT[0][:, c0[d]:c0[d] + 128].unsqueeze(1).broadcast_to([128, 2, 128]),
                   hm.unsqueeze(2).broadcast_to([128, 2, 128]), ALU.mult, [xT_tr[0], cb_tr], [b5t])
            bk, bk_tr = bank()
            pbt = bk[:, 0:384].bitcast(BF16).rearrange("p (a b) -> p a b", b=128)
            for j in range(3):
                for d in range(2):
                    tp(pbt[:, j * 2 + d, :], xT[j][:, c0[d]:c0[d] + 128], [xT_tr[j]], [bk_tr])
            yield
            srcK = pbt[:, 0:2, :].rearrange("p d (h c) -> p d h c", h=2)
            srcQ = pbt[:, 2:4, :].rearrange("p d (h c) -> p d h c", h=2)
            srcV = pbt[:, 4:6, :].rearrange("p d (h c) -> p d h c", h=2)
            for dst, src, sarr, sarr_tr, k_ in ((Qg, srcQ, sQg, sQg_tr, 2), (Kbg, srcK, sKbg, sKbg_tr, 0), (Kd_, srcK, sKd, sKd_tr, 1),
                                                 (Vb, srcV, beta, beta_tr, 3)):
                tt("dve", dst.rearrange("p (d h) c -> p d h c", d=2), src,
                   scal(sarr).unsqueeze(3).broadcast_to([128, 2, 2, 64]), ALU.mult, [bk_tr, sarr_tr], [tok_tr[k_]])
            bk, bk_tr = bank()
            pq = bk[0:64, 0:256].bitcast(BF16).rearrange("p (a b) -> p a b", b=128)
            for u in range(4):
                tp(pq[:, u, :], Qg[:, u, :], [tok_tr[2]], [bk_tr])
            yield
            cp("act", QgT[0:64], pq, [bk_tr], [QgT_tr])
            bk, bk_tr = bank()
            for d in range(2):
                us = slice(d * 2, d * 2 + 2)
                o = bk[:, d * 256:(d + 1) * 256]
                mm(o, cb[:, CB_L + d, :], b1[:, us, :], True, False, [b1t, cb_tr], [bk_tr])
                mm(o, ident, b3[:, us, :], False, True, [b3t, cb_tr], [bk_tr])
            yield
            act(f2(e0), bk[:, :], AF.Exp, [bk_tr], [e0t])
            bk, bk_tr = bank()
            for d in range(2):
                us = slice(d * 2, d * 2 + 2)
                o = bk[:, d * 256:(d + 1) * 256]
                mm(o, cb[:, CB_SM + d, :], b2[:, us, :], True, False, [b2t, cb_tr], [bk_tr])
                mm(o, ones_b, b4[:, us, :], False, False, [b4t, cb_tr], [bk_tr])
                mm(o, ident, cb[:, CB_NEGST + 2 * d:CB_NEGST + 2 * d + 2, :], False, True, [cb_tr], [bk_tr])
            yield
            act(f2(e1), bk[:, :], AF.Exp, [bk_tr], [e1t])
            bk, bk_tr = bank()
            for d in range(2):
                us = slice(d * 2, d * 2 + 2)
                o = bk[:, d * 256:(d + 1) * 256]
                mm(o, cb[:, CB_SM + d, :], b2[:, us, :], True, False, [b2t, cb_tr], [bk_tr])
                mm(o, ident, cb[:, CB_NEGIT + 2 * d:CB_NEGIT + 2 * d + 2, :], False, True, [cb_tr], [bk_tr])
            yield
            act(f2(e2), bk[:, :], AF.Exp, [bk_tr], [e2t])
            bk, bk_tr = bank()
            for d in range(2):
                for h in range(2):
                    u = d * 2 + h
                    mm(u3(bk)[:, u, :], b5[:, u, :], xT[0][:, c0[d]:c0[d] + 128], True, True, [xT_tr[0], b5t], [bk_tr])
            yield
            stt(f2(e0), bk[:, :], -1.0, f2(e0), ALU.mult, ALU.mult, [bk_tr, e0t], [e0t])
            stt(f2(e1), bk[:, :], -1.0, f2(e1), ALU.mult, ALU.mult, [bk_tr, e1t], [e1t])
            bk, bk_tr = bank()
            for d in range(2):
                for h in range(2):
                    u = d * 2 + h
                    mm(u3(bk)[:, u, :], b5[:, u, :], xT[1][:, c0[d]:c0[d] + 128], True, True, [xT_tr[1], b5t], [bk_tr])
            tt("pool", p0, e0, cb[:, CB_BONES1, :].unsqueeze(1).broadcast_to([128, 4, 128]), ALU.mult, [e0t, cb_tr], [p0t])
            tt("pool", v4(bo), v4(e0), m2(CB_OFF), ALU.mult, [e0t, cb_tr], [bot])
            tt("pool", x0, e1, identb4, ALU.add, [e1t, cb_tr], [x0t])
            yield
            stt(f2(e2), bk[:, :], 0.125, f2(e2), ALU.mult, ALU.mult, [bk_tr, e2t], [e2t])
            Pp, Ppt = [p0, b4], [p0t, b4t]
            PT, PTt = [e1, b5], [e1t, b5t]
            XT, XTt = [x0, x1_], [x0t, x1t]
            cur = 0
            for k in range(5):
                nxt = 1 - cur
                bk, bk_tr = bank()
                for u in range(4):
                    mm(u3(bk)[:, u, :], PT[cur][:, u, :], Pp[cur][:, u, :], True, True, [PTt[cur], Ppt[cur]], [bk_tr])
                yield
                cp("act", f2(Pp[nxt]), bk[:, :], [bk_tr], [Ppt[nxt]])
                if k < 4:
                    bk, bk_tr = bank()
                    for u in range(4):
                        mm(u3(bk)[:, u, :], Pp[cur][:, u, :], PT[cur][:, u, :], True, True, [PTt[cur], Ppt[cur]], [bk_tr])
                    yield
                    cp("act" if k % 2 == 0 else "dve", f2(PT[nxt]), bk[:, :], [bk_tr], [PTt[nxt]])
                bk, bk_tr = bank()
                for u in range(4):
                    mm(u3(bk)[:, u, :], Pp[nxt][:, u, :], XT[cur][:, u, :], True, True, [Ppt[nxt], XTt[cur]], [bk_tr])
                yield
                tt("dve", f2(XT[nxt]), bk[:, :], f2(XT[cur]), ALU.add, [bk_tr, XTt[cur]], [XTt[nxt]])
                cur = nxt
            nxt = 1 - cur
            bk, bk_tr = bank()
            for u in range(4):
                mm(u3(bk)[:, u, :], bo[:, u, :], XT[cur][:, u, :], True, True, [bot, XTt[cur]], [bk_tr])
            yield
            cp("act", f2(b1), bk[:, :], [bk_tr], [b1t])
            bk, bk_tr = bank()
            pxd = bk[:, 0:256].bitcast(BF16).rearrange("p (a b) -> p a b", b=128)
            for u in range(4):
                tp(pxd[:, u, :], XT[cur][:, u, :], [XTt[cur]], [bk_tr])
            yield
            cp("act", b2, pxd, [bk_tr], [b2t])
            bk, bk_tr = bank()
            for u in range(4):
                mm(u3(bk)[:, u, :], b2[:, u, :], b1[:, u, :], True, True, [b2t, b1t], [bk_tr])
            yield
            tt("dve", f2(XT[nxt]), bk[:, :], f2(XT[cur]), ALU.add, [bk_tr, XTt[cur]], [XTt[nxt]])
            TT, TT_tr = XT[nxt], XTt[nxt]
            bk, bk_tr = bank()
            uu = bk[:, 0:256].rearrange("p (u c) -> p u c", c=64)
            for u in range(4):
                mm(uu[:, u, :], TT[:, u, :], Vb[:, u, :], True, True, [TT_tr, tok_tr[3]], [bk_tr])
            yield
            U_sb = e0.rearrange("p u c -> p (u c)").bitcast(F32).rearrange("p (u c) -> p u c", c=64)
            cp("act", U_sb, uu, [bk_tr], [e0t])
            bk, bk_tr = bank()
            w3 = bk[0:64, :].rearrange("p (u c) -> p u c", c=128)
            for u in range(4):
                mm(w3[:, u, :], Kbg[:, u, :], TT[:, u, :], True, True, [TT_tr, tok_tr[0]], [bk_tr])
            yield
            cp("dve", b3[0:64], w3, [bk_tr], [b3t])
            B["r"] = dict(Kd_=Kd_, Kd_tr=tok_tr[1], QgT=QgT, QgT_tr=QgT_tr, QKm=e2, QKm_tr=e2t, U_sb=U_sb, U_tr=e0t, WT=b3, WT_tr=b3t)

        def step_recur(n, B):
            r_ = B["r"]
            Kd_, Kd_tr, QgT, QgT_tr, QKm, QKm_tr, U_sb, U_tr, WT, WT_tr = (r_[k_] for k_ in (
                "Kd_", "Kd_tr", "QgT", "QgT_tr", "QKm", "QKm_tr", "U_sb", "U_tr", "WT", "WT_tr"))
            tl = (n, NT - 1 - n)
            hs = slice(hp * 2, hp * 2 + 2)
            bk1, bk1_tr = bank()
            b13 = bk1[:, 0:256].rearrange("p (u c) -> p u c", c=64)
            for u in range(4):
                mm(b13[:, u, :], WT[0:64, u, :], Sbf[0:64, u, :], True, True, [WT_tr, Sbf_tr], [bk1_tr])
            tt("dve", Vn, U_sb, b13, ALU.subtract, [U_tr, bk1_tr], [Vn_tr])
            bkO, bkO_tr = bank()
            o3 = bkO[:, 0:256].rearrange("p (u c) -> p u c", c=64)
            for u in range(4):
                mm(o3[:, u, :], QgT[0:64, u, :], Sbf[0:64, u, :], True, False, [QgT_tr, Sbf_tr], [bkO_tr])
                mm(o3[:, u, :], QKm[:, u, :], Vn[:, u, :], False, True, [QKm_tr, Vn_tr], [bkO_tr])
            for d in range(2):
                if n < NT // 2:
                    cp("act", Oacc[:, tl[d], :], bkO[:, d * 128:(d + 1) * 128], [bkO_tr], [Oacc_tr])
                else:
                    tt("dve", Oacc[:, tl[d], :], bkO[:, d * 128:(d + 1) * 128], Oacc[:, tl[d], :], ALU.add, [bkO_tr, Oacc_tr], [Oacc_tr])
            bkS, bkS_tr = bank()
            s3 = bkS[0:64, 0:256].rearrange("p (u c) -> p u c", c=64)
            for u in range(4):
                mm(s3[:, u, :], Kd_[:, u, :], Vn[:, u, :], True, True, [Kd_tr, Vn_tr], [bkS_tr])
            dsc = dec.rearrange("p n (d h) -> p n d h", d=2)[0:64, n, :, hs].unsqueeze(3).broadcast_to([64, 2, 2, 64])
            tt("pool", Sd[0:64].rearrange("p (d h) c -> p d h c", d=2), S32[0:64].rearrange("p (d h) c -> p d h c", d=2),
               dsc, ALU.mult, [S32_tr, dec_tr], [Sd_tr])
            tt("dve", S32[0:64], Sd[0:64], s3, ALU.add, [Sd_tr, bkS_tr], [S32_tr])
            cp("act", Sbf[0:64], S32[0:64], [S32_tr], [Sbf_tr])

        NSET = len(sets)
        STAG = 7
        nst = NT if debug is None else debug.get('nsteps', NT)
        active = []
        free_sets = list(range(NSET))
        next_n, tick, last_done = 0, 0, -1
        while next_n < nst or active:
            if next_n < nst and free_sets and (tick % STAG == 0 or not active):
                si = free_sets.pop(0)
                active.append([step_setup(next_n, sets[si]), next_n, si])
                next_n += 1
            for item in list(active):
                try:
                    next(item[0])
                except StopIteration:
                    active.remove(item)
                    assert item[1] == last_done + 1
                    last_done = item[1]
                    step_recur(item[1], sets[item[2]])
                    free_sets.append(item[2])
            tick += 1

        if debug is not None and debug["what"] == "oacc" and hp == debug.get("hp", 0):
            return dump([(Oacc.rearrange("p n c -> p (n c)"), Oacc_tr)], 128)

        barrier_all('scan_end')
        for gq in range(8):
            bk, bk_tr = bank()
            z3 = bk[:, :].rearrange("p (t c) -> p t c", c=128)
            for t_ in range(4):
                tl_ = gq * 4 + t_
                for kc in range(KC):
                    mm(z3[:, t_, :], hT[:, kc, 1 + tl_ * 128:1 + (tl_ + 1) * 128], wq[:, kc, 3, :], kc == 0, kc == KC - 1,
                       [hT_tr[tl_], wq_tr], [bk_tr])
            act(f2(szb), bk[:, :], AF.Silu, [bk_tr], [szb_tr])
            cp("pool", ob, Oacc[:, gq * 4:gq * 4 + 4, :], [Oacc_tr], [ob_tr])
            tt("pool", osq, ob, ob, ALU.mult, [ob_tr], [osq_tr])
            P.op("dve", lambda e: e.tensor_reduce(out=ost[:, 0:8], in_=f2(osq).rearrange("p (a c) -> p a c", c=64),
                                                  axis=AX.X, op=ALU.add), [osq_tr], [ost_tr])
            act(ost[:, 0:8], ost[:, 0:8], AF.Sqrt, [ost_tr], [ost_tr], bias=EPS, scale=1.0 / 64)
            P.op("dve", lambda e: e.reciprocal(out=ost[:, 8:16], in_=ost[:, 0:8]), [ost_tr], [ost_tr])
            o4 = f2(ob).rearrange("p (a c) -> p a c", c=64)
            tt("dve", o4, o4, ost[:, 8:16].unsqueeze(2).broadcast_to([128, 8, 64]), ALU.mult, [ob_tr, ost_tr], [ob_tr])
            tt("pool", o4, o4, onw[:, 0:64].unsqueeze(1).broadcast_to([128, 8, 64]), ALU.mult, [ob_tr, onw_tr], [ob_tr])
            tt("dve", ogb, ob, szb, ALU.mult, [ob_tr, szb_tr], [ogb_tr])
            bk, bk_tr = bank()
            pt4 = bk[:, 0:256].bitcast(BF16).rearrange("p (a b) -> p a b", b=128)
            for t_ in range(4):
                tp(pt4[:, t_, :], ogb[:, t_, :], [ogb_tr], [bk_tr])
            cp("act", oaT_stage[:, gq * 512:(gq + 1) * 512], bk[:, 0:256].bitcast(BF16), [bk_tr], [oaT_stage_tr])
        dma(oaT_d[hp * 128:(hp + 1) * 128, :], oaT_stage, reads=[oaT_stage_tr], writes=[oaT_tr])
        barrier_all('A4_end')

    if debug is not None and debug["what"] == "oaT":
        stg2 = Alloc(PERS_END).get([S], BF16)
        stg2_tr = Tr()
        out_list = []
        for r in range(4):
            dma(stg2, oaT_d[r * 128:(r + 1) * 128, :], reads=[oaT_tr], writes=[stg2_tr])
            stg = Alloc(PERS_END + 16384).get([S], F32)
            stg_tr = Tr()
            cp("dve", stg, stg2, [stg2_tr], [stg_tr])
            dma(dbg_d[r * 128:(r + 1) * 128, :], stg, reads=[stg_tr])
            barrier_all()
        P.finish()
        P.emit()
        return nc

    PAD = 64
    pbl = Alloc(PERS_END)
    wstBs = [pbl.get([KC * 256], F32) for _ in range(2)]; wstBs_tr = [Tr(), Tr()]
    wBs = [[pbl.get([KC, 256], BF16) for _ in range(3)] for _ in range(2)]
    wBs_tr = [[Tr(), Tr(), Tr()] for _ in range(2)]
    ws_i = [0]
    qTg = pbl.get([2, PAD + S + PAD], BF16); qTg_tr = Tr()
    kTg = pbl.get([2, PAD + S + PAD], BF16); kTg_tr = Tr()
    Vaug = pbl.get([NT, 4, 65], BF16); Vaug_tr = Tr()
    amks = [pbl.get([4, 2, 128], F32) for _ in range(2)]; amks_tr = [Tr(), Tr()]
    sqBs = [pbl.get([512], BF16) for _ in range(4)]; sqBs_tr = trs(4)
    rnBs = [pbl.get([512], F32) for _ in range(4)]; rnBs_tr = trs(4)
    nb_i = [0]
    exB = [pbl.get([512], F32) for _ in range(4)]; exB_tr = trs(4)
    ptB = [[[pbl.get([4, 128], BF16) for _ in range(2)] for _ in range(4)] for _ in range(2)]
    ptB_tr = [[[Tr(), Tr()] for _ in range(4)] for _ in range(2)]
    ostg = [pbl.get([4, 65], BF16) for _ in range(2)]; ostg_tr = [Tr(), Tr()]
    oB_tr = [Tr() for _ in range(3)]

    sc_banks = {0: (0,), 1: (1,)}
    sc_rr = {0: 0, 1: 0}
    wk_banks = (2, 3, 4, 5, 6, 7)
    wk_rr = [0]

    def bank_sc(par):
        i = sc_banks[par][sc_rr[par] % len(sc_banks[par])]
        sc_rr[par] += 1
        return banks[i], bank_tr[i]

    def bank_wk():
        i = wk_banks[wk_rr[0] % len(wk_banks)]
        wk_rr[0] += 1
        return banks[i], bank_tr[i]

    memset("pool", qTg, 0.0, [qTg_tr])
    memset("pool", kTg, 0.0, [kTg_tr])
    memset("pool", Vaug, 1.0, [Vaug_tr])
    ex_i = [0]
    os_i = [0]
    def load_attn_w(g_, hh_):
        it2 = (g_ * 2 + hh_) % 2
        for wi, cbase in enumerate((C_QB, C_KB, C_VB)):
            col = cbase + g_ * 512 + hh_ * 256
            wstB, wstB_tr = wstBs[ws_i[0] % 2], wstBs_tr[ws_i[0] % 2]
            ws_i[0] += 1
            sv = wstB[:, 0:KC * 256].rearrange("p (k c) -> p k c", c=256)
            dma(sv, w_in[:, col:col + 256].rearrange("(k p) c -> p k c", p=128), writes=[wstB_tr])
            cp("pool", wBs[it2][wi], sv, [wstB_tr], [wBs_tr[it2][wi]])
        dma(amks[it2].rearrange("p h a c -> p (h a c)"), am_d[g_, :, hh_ * 1024:(hh_ + 1) * 1024], writes=[amks_tr[it2]])

    for g, (_, dil) in enumerate(ATT_GROUPS):
        L = S // dil
        NB = min(512, L)
        for hh in range(2):
            it_ = (g * 2 + hh) % 2
            wqB, wkB, wvB = wBs[it_]
            wB_tr = wBs_tr[it_]
            amk, amk_tr = amks[it_], amks_tr[it_]
            if g == 0 and hh == 0:
                load_attn_w(0, 0)
            units = [(which, hp2, un) for which in range(2) for hp2 in range(2) for un in range(S // 512)]
            qk_info = ((qTg, qTg_tr, wqB, 16), (kTg, kTg_tr, wkB, 17))
            ust = {}

            def qk_proj(ui):
                which, hp2, un = units[ui]
                dst, dst_tr, wt_, wcol = qk_info[which]
                bk, bk_tr = bank_wk()
                for kc in range(KC):
                    mm(bk[:, :], wt_[:, kc, hp2 * 128:(hp2 + 1) * 128], hT[:, kc, 1 + un * 512:1 + (un + 1) * 512], kc == 0,
                       kc == KC - 1, [wB_tr[which]] + hT_tr[un * 4:un * 4 + 4], [bk_tr])
                i4 = nb_i[0] % 4
                nb_i[0] += 1
                act(sqBs[i4], bk[:, :], AF.Square, [bk_tr], [sqBs_tr[i4]])
                ust[ui] = (bk, bk_tr, i4)

            def qk_norm(ui):
                which, hp2, un = units[ui]
                dst, dst_tr, wt_, wcol = qk_info[which]
                bk, bk_tr, i4 = ust.pop(ui)
                bk2, bk2_tr = bank_wk()
                mm(bk2[:, :], cb[:, CB_BONES, :], sqBs[i4], True, True, [sqBs_tr[i4], cb_tr], [bk2_tr])
                act(rnBs[i4], bk2[:, :], AF.Ln, [bk2_tr], [rnBs_tr[i4]], bias=EPS)
                act(rnBs[i4], rnBs[i4], AF.Exp, [rnBs_tr[i4]], [rnBs_tr[i4]], scale=-0.5)
                mb = 512 // dil
                m_base = (un * 512) // dil
                o_ap = dst[:, hp2, PAD:PAD + S].rearrange("p (r l) -> p r l", l=L)[:, :, m_base:m_base + mb]
                i0_ap = bk[:, :].rearrange("p (m r) -> p r m", r=dil)
                i1_ap = rnBs[i4].rearrange("p (m r) -> p r m", r=dil)
                stt(o_ap, i0_ap, vec[:, wcol:wcol + 1], i1_ap, ALU.mult, ALU.mult, [bk_tr, vec_tr, rnBs_tr[i4]], [dst_tr])

            qk_proj(0)
            for ui in range(len(units)):
                if ui + 1 < len(units):
                    qk_proj(ui + 1)
                qk_norm(ui)
            if debug is not None and debug['what'] == 'qkB' and g == debug.get('g', 0) and hh == 0:
                return dump([(qTg[:, 0, PAD:PAD + S], qTg_tr), (kTg[:, 0, PAD:PAD + S], kTg_tr)], 128)
            mark('B%d%d_qk' % (g, hh))
            for pt_ in range(NT):
                p0 = pt_ * 128
                r, m0 = p0 // L, p0 % L
                c_lo = 1 + r + dil * m0
                c_hi = c_lo + dil * 127 + 1
                tiles = sorted(set(range((c_lo - 1) // 128, (c_hi - 2) // 128 + 1)))
                bk, bk_tr = bank_wk()
                for kc in range(KC):
                    mm(bk[:, 0:256], hT[:, kc, c_lo:c_hi:dil], wvB[:, kc, :], kc == 0, kc == KC - 1,
                       [wB_tr[2]] + [hT_tr[t_] for t_ in tiles], [bk_tr])
                cp("act", Vaug[:, pt_, :, 0:64], bk[:, 0:256].rearrange("p (h c) -> p h c", c=64), [bk_tr], [Vaug_tr])
            mark('B%d%d_v' % (g, hh))
            nxt_it = g * 2 + hh + 1
            if nxt_it < 6:
                load_attn_w(nxt_it // 2, nxt_it % 2)
            ntl = L // 128
            qblocks = [(r, c) for r in range(dil) for c in range(-1, ntl)]
            def scores(batch, bp):
                nq = len(batch)
                for hp2, ab, par in [(a_, b_, c_) for a_ in range(2) for b_ in range(2) for c_ in range(2)]:
                    h4 = hp2 * 2 + par
                    rows = slice(par * 64, par * 64 + 64)
                    if True:
                        bk, bk_tr = bank_sc(par)
                        s3_ = bk[:, :].rearrange("p (q c) -> p q c", c=128)
                        for qi, (r, c) in enumerate(batch):
                            kt = c + ab
                            if kt < 0 or kt >= ntl:
                                continue
                            kcol = PAD + r * L + kt * 128
                            qcol = PAD + r * L + 64 + 128 * c
                            mm(s3_[:, qi, :], kTg[rows, hp2, kcol:kcol + 128], qTg[rows, hp2, qcol:qcol + 128], True, True,
                               [kTg_tr, qTg_tr], [bk_tr])
                        ei = ex_i[0] % 4
                        ex_i[0] += 1
                        act(exB[ei][:, 0:nq * 128], bk[:, 0:nq * 128], AF.Exp, [bk_tr], [exB_tr[ei]], scale=0.125)
                        tt("dve", ptB[bp][h4][ab][:, 0:nq, :], exB[ei][:, 0:nq * 128].rearrange("p (q c) -> p q c", c=128),
                           amk[:, h4, ab, :].unsqueeze(1).broadcast_to([128, nq, 128]), ALU.mult, [exB_tr[ei], amk_tr],
                           [ptB_tr[bp][h4][ab]])

            def pv(batch, bp):
                for qi, (r, c) in enumerate(batch):
                    bk, bk_tr = bank_wk()
                    o3_ = bk[:, 0:260].rearrange("p (h c) -> p h c", c=65)
                    va, vb_ = (c >= 0), (c + 1 < ntl)
                    for h4 in range(4):
                        if va:
                            mm(o3_[:, h4, :], ptB[bp][h4][0][:, qi, :], Vaug[:, (r * L) // 128 + c, h4, :], True, not vb_,
                               [ptB_tr[bp][h4][0], Vaug_tr], [bk_tr])
                        if vb_:
                            mm(o3_[:, h4, :], ptB[bp][h4][1][:, qi, :], Vaug[:, (r * L) // 128 + c + 1, h4, :], not va, True,
                               [ptB_tr[bp][h4][1], Vaug_tr], [bk_tr])
                    oi = os_i[0] % 2
                    os_i[0] += 1
                    cp("act", ostg[oi], o3_, [bk_tr], [ostg_tr[oi]])
                    plo, phi = (64 if c < 0 else 0), (64 if c + 1 >= ntl else 128)
                    m_lo = 64 + 128 * c + plo
                    oview = oB_d[g].rearrange("(m d) c -> d m c", d=dil)
                    dma(oview[r, m_lo:m_lo + (phi - plo), hh * 260:(hh + 1) * 260],
                        ostg[oi][plo:phi].rearrange("p h c -> p (h c)"), reads=[ostg_tr[oi]], writes=[oB_tr[g]])

            prev = None
            for bi, b0 in enumerate(range(0, len(qblocks), 4)):
                batch = qblocks[b0:b0 + 4]
                scores(batch, bi % 2)
                if prev is not None:
                    pv(*prev)
                prev = (batch, bi % 2)
            pv(*prev)
            mark('B%d%d_att' % (g, hh))
    barrier_all('B_end')

    if debug is not None and debug["what"] == "oB":
        da_ = Alloc(PERS_END)
        stg = da_.get([520], F32); stg_tr = Tr()
        for g_ in range(3):
            for t_ in range(NT):
                dma(stg, oB_d[g_][t_ * 128:(t_ + 1) * 128, :], reads=[oB_tr[g_]], writes=[stg_tr])
                dma(dbg_d[g_ * S + t_ * 128:g_ * S + (t_ + 1) * 128, :], stg, reads=[stg_tr])
        P.finish()
        P.emit()
        return nc

    pc = Alloc(PERS_END)
    STG = 3072
    wst = pc.get([STG], F32); wst_tr = Tr()
    Wga = pc.get([KC, 1024], BF16); Wgb = pc.get([KC, 1024], BF16); Wo_ = pc.get([KC, 1024], BF16)
    Wdno = pc.get([4, 1024], BF16); Wato = pc.get([4, 1024], BF16)
    Wc_tr = [Tr() for _ in range(5)]

    lb_i = [0]

    def load_big(dst, dst_tr, src, nkc, ncols, stage, stage_tr, cast_eng=None):
        cw = 512 if ncols >= 512 else ncols
        kstep = max(1, STG // cw)
        for c0_ in range(0, ncols, cw):
            w_ = min(cw, ncols - c0_)
            for k0 in range(0, nkc, kstep):
                nk = min(kstep, nkc - k0)
                sv = stage[:, 0:nk * w_].rearrange("p (k c) -> p k c", c=w_)
                dma(sv, src[k0 * 128:(k0 + nk) * 128, c0_:c0_ + w_].rearrange("(k p) c -> p k c", p=128), writes=[stage_tr])
                lb_i[0] += 1
                cp(cast_eng or ("act" if lb_i[0] % 2 else "dve"), dst[:, k0:k0 + nk, c0_:c0_ + w_], sv, [stage_tr], [dst_tr])

    Wc2_tr = [[Tr(), Tr()] for _ in range(5)]
    wsth = [wst[:, 0:STG // 2], wst[:, STG // 2:STG]]
    wsth_tr = [Tr(), Tr()]
    wc_i = [0]

    def load_chunk(dst, wi, src, nkc, ch_):
        c0_ = ch_ * 512
        for k0 in range(0, nkc, 3):
            nk = min(3, nkc - k0)
            hi = wc_i[0] % 2
            wc_i[0] += 1
            sv = wsth[hi][:, 0:nk * 512].rearrange("p (k c) -> p k c", c=512)
            dma(sv, src[k0 * 128:(k0 + nk) * 128, c0_:c0_ + 512].rearrange("(k p) c -> p k c", p=128), writes=[wsth_tr[hi]])
            cp("act" if wc_i[0] % 2 else "dve", dst[:, k0:k0 + nk, c0_:c0_ + 512], sv, [wsth_tr[hi]], [Wc2_tr[wi][ch_]])

    for ch_ in range(2):
        load_chunk(Wdno, 3, w_dno, 4, ch_)
        load_chunk(Wato, 4, w_ato, 4, ch_)
        load_chunk(Wga, 0, w_in[:, C_GA:C_GA + 1024], KC, ch_)
        load_chunk(Wgb, 1, w_in[:, C_GB:C_GB + 1024], KC, ch_)
        if ch_ == 0:
            load_chunk(Wo_, 2, w_o, KC, 0)
    load_chunk(Wo_, 2, w_o, KC, 1)

    oBt = [pc.get([8, 65], BF16) for _ in range(3)]; oBt_tr = [Tr() for _ in range(3)]
    osum = pc.get([8, 65], F32); osum_tr = Tr()
    rden = pc.get([8], F32); rden_tr = Tr()
    obn = pc.get([8, 64], BF16); obn_tr = Tr()
    obTs = [pc.get([4, 512], BF16) for _ in range(2)]; obTs_tr = [Tr(), Tr()]
    oaTbs = [pc.get([4, 512], BF16) for _ in range(2)]; oaTbs_tr = [Tr(), Tr()]
    sga = pc.get([512], BF16); sga_tr = Tr()
    sgb = pc.get([512], BF16); sgb_tr = Tr()
    mixT = pc.get([8, 512], BF16); mixT_tr = Tr()
    xsC = [pc.get([D], F32) for _ in range(2)]; xsC_tr = [Tr(), Tr()]
    x1t = [pc.get([D], F32) for _ in range(2)]; x1t_tr = [Tr(), Tr()]
    sqC = pc.get([D], BF16); sqC_tr = Tr()
    stC = [pc.get([8], F32) for _ in range(2)]; stC_tr = [Tr(), Tr()]
    xnC = [pc.get([D], BF16) for _ in range(2)]; xnC_tr = [Tr(), Tr()]
    scrC = (sqC, sqC_tr, stC, stC_tr, xnC, xnC_tr)
    x1_tr = Tr()
    oaT_v = oaT_d.rearrange("(k p) s -> p k s", p=128)

    def c_stage1(tb):
        obT, obT_tr, oaTb, oaTb_tr = obTs[tb % 2], obTs_tr[tb % 2], oaTbs[tb % 2], oaTbs_tr[tb % 2]
        for t_ in range(4):
            tile = tb * 4 + t_
            for g in range(3):
                dma(oBt[g].rearrange("p h c -> p (h c)"), oB_d[g][tile * 128:(tile + 1) * 128, :], reads=[oB_tr[g]],
                    writes=[oBt_tr[g]])
            tt("pool", osum, oBt[0], oBt[1], ALU.add, [oBt_tr[0], oBt_tr[1]], [osum_tr])
            tt("pool", osum, osum, oBt[2], ALU.add, [osum_tr, oBt_tr[2]], [osum_tr])
            P.op("dve", lambda e: e.reciprocal(out=rden, in_=osum[:, :, 64]), [osum_tr], [rden_tr])
            tt("dve", obn, osum[:, :, 0:64], rden.unsqueeze(2).broadcast_to([128, 8, 64]), ALU.mult, [osum_tr, rden_tr],
               [obn_tr])
            bk, bk_tr = bank()
            pt4 = bk[:, 0:256].bitcast(BF16).rearrange("p (a b) -> p a b", b=128)
            for hp in range(4):
                tp(pt4[:, hp, :], obn[:, 2 * hp:2 * hp + 2, :].rearrange("p h c -> p (h c)"), [obn_tr], [bk_tr])
            cp("act", obT[:, :, t_ * 128:(t_ + 1) * 128], pt4, [bk_tr], [obT_tr])
        dma(oaTb, oaT_v[:, :, tb * 512:(tb + 1) * 512], reads=[oaT_tr], writes=[oaTb_tr])

    pend_c = [None]
    c_stage1(0)
    for tb in range(8):
        if tb + 1 < 8:
            c_stage1(tb + 1)
        obT, obT_tr, oaTb, oaTb_tr = obTs[tb % 2], obTs_tr[tb % 2], oaTbs[tb % 2], oaTbs_tr[tb % 2]
        hts = hT_tr[tb * 4:tb * 4 + 4]
        for cbk in range(8):
            cs = slice(cbk * 128, (cbk + 1) * 128)
            bya, bya_tr = bank()
            byb, byb_tr = bank()
            bga, bga_tr = bank()
            bgb, bgb_tr = bank()
            for kc in range(4):
                mm(bya[:, :], Wdno[:, kc, cs], oaTb[:, kc, :], kc == 0, kc == 3, [Wc2_tr[3][cbk // 4], oaTb_tr], [bya_tr])
            for kc in range(4):
                mm(byb[:, :], Wato[:, kc, cs], obT[:, kc, :], kc == 0, kc == 3, [Wc2_tr[4][cbk // 4], obT_tr], [byb_tr])
            for kc in range(KC):
                mm(bga[:, :], Wga[:, kc, cs], hT[:, kc, 1 + tb * 512:1 + (tb + 1) * 512], kc == 0, kc == KC - 1,
                   [Wc2_tr[0][cbk // 4]] + hts, [bga_tr])
            for kc in range(KC):
                mm(bgb[:, :], Wgb[:, kc, cs], hT[:, kc, 1 + tb * 512:1 + (tb + 1) * 512], kc == 0, kc == KC - 1,
                   [Wc2_tr[1][cbk // 4]] + hts, [bgb_tr])
            act(sga, bga[:, :], AF.Sigmoid, [bga_tr], [sga_tr])
            act(sgb, bgb[:, :], AF.Sigmoid, [bgb_tr], [sgb_tr])
            tt("dve", sga, bya[:, :], sga, ALU.mult, [bya_tr, sga_tr], [sga_tr])
            tt("dve", sgb, byb[:, :], sgb, ALU.mult, [byb_tr, sgb_tr], [sgb_tr])
            tt("pool", mixT[:, cbk, :], sga, sgb, ALU.add, [sga_tr, sgb_tr], [mixT_tr])
        dma(xsC[(tb * 4) % 2], x_d[tb * 4 * 128:(tb * 4 + 1) * 128, :], writes=[xsC_tr[(tb * 4) % 2]], eng="pool")
        for t_ in range(4):
            tile = tb * 4 + t_
            i2 = tile % 2
            if t_ + 1 < 4:
                dma(xsC[1 - i2], x_d[(tile + 1) * 128:(tile + 2) * 128, :], writes=[xsC_tr[1 - i2]], eng="pool")
            for half in range(2):
                bk, bk_tr = bank()
                for cbk in range(8):
                    mm(bk[:, :], mixT[:, cbk, t_ * 128:(t_ + 1) * 128], Wo_[:, cbk, half * 512:(half + 1) * 512], cbk == 0, cbk == 7,
                       [mixT_tr, Wc2_tr[2][half]], [bk_tr])
                tt("dve", x1t[i2][:, half * 512:(half + 1) * 512], bk[:, :], xsC[i2][:, half * 512:(half + 1) * 512], ALU.add,
                   [bk_tr, xsC_tr[i2]], [x1t_tr[i2]])
            dma(x1_d[tile * 128:(tile + 1) * 128, :], x1t[i2], reads=[x1t_tr[i2]], writes=[x1_tr])
            nb_ = norm_tile(x1t[i2], x1t_tr[i2], tile, i2, 8, scrC, defer=True)
            if pend_c[0] is not None:
                pend_c[0]()
            pend_c[0] = nb_

    if pend_c[0] is not None:
        pend_c[0]()
        pend_c[0] = None
    if debug is not None and debug["what"] == "x1":
        barrier_all()
        for t_ in range(NT):
            dma(xsC[0], x1_d[t_ * 128:(t_ + 1) * 128, :], reads=[x1_tr], writes=[xsC_tr[0]])
            dma(dbg_d[t_ * 128:(t_ + 1) * 128, :], xsC[0], reads=[xsC_tr[0]])
        P.finish()
        P.emit()
        return nc
    barrier_all('C_end')

    pd = Alloc(PERS_END)
    PASSES = [list(range(0, 8)), list(range(8, 15)), list(range(15, 22))]
    NCH = 8
    wstD = [pd.get([1024], F32) for _ in range(3)]; wstD_tr = trs(3)
    wupS = [pd.get([KC, 2 * NCH, 128], BF16) for _ in range(2)]; wupS_tr = [Tr(), Tr()]
    wdnS = [pd.get([NCH, 1024], BF16) for _ in range(2)]; wdnS_tr = [Tr(), Tr()]
    fcw = pd.get([44, 3], F32); fcw_tr = Tr()
    cg = [pd.get([256], F32) for _ in range(2)]; cg_tr = [Tr(), Tr()]
    cu = [pd.get([256], F32) for _ in range(2)]; cu_tr = [Tr(), Tr()]
    ab_ = [pd.get([256], BF16) for _ in range(2)]; ab_tr_ = [Tr(), Tr()]
    xr = [pd.get([D], F32) for _ in range(2)]; xr_tr = [Tr(), Tr()]
    out_tr = Tr()
    dma(fcw, f_cw, writes=[fcw_tr])
    wrk_rr = [0]

    def bank_lo():
        i = wrk_rr[0] % 4
        wrk_rr[0] += 1
        return banks[i], bank_tr[i]

    wp_i = [0]

    def weight_pieces(ps):
        bs = ps % 2
        out = []
        for c, ch in enumerate(PASSES[ps]):
            for kind in range(3):
                def mk(c=c, ch=ch, kind=kind):
                    st_ = {}

                    def issue():
                        i3 = wp_i[0] % 3
                        wp_i[0] += 1
                        st_["i3"] = i3
                        if kind < 2:
                            col = kind * DFF + ch * 128
                            sv = wstD[i3].rearrange("p (k c) -> p k c", c=128)
                            dma(sv, w_up[:, col:col + 128].rearrange("(k p) c -> p k c", p=128), writes=[wstD_tr[i3]])
                        else:
                            dma(wstD[i3], w_dn[ch * 128:(ch + 1) * 128, :], writes=[wstD_tr[i3]])

                    def cast(eng):
                        i3 = st_["i3"]
                        if kind < 2:
                            sv = wstD[i3].rearrange("p (k c) -> p k c", c=128)
                            cp(eng, wupS[bs][:, :, kind * NCH + c, :], sv, [wstD_tr[i3]], [wupS_tr[bs]])
                        else:
                            cp(eng, wdnS[bs][:, c, :], wstD[i3], [wstD_tr[i3]], [wdnS_tr[bs]])
                    return issue, cast
                out.append(mk())
        return out

    pcs = weight_pieces(0)
    for i_, (iss, cst) in enumerate(pcs):
        iss()
        if i_ >= 2:
            pcs[i_ - 2][1]("act" if i_ % 2 else "dve")
    for i_ in range(len(pcs) - 2, len(pcs)):
        pcs[i_][1]("act" if i_ % 2 else "dve")

    ci = 0
    for ps in range(len(PASSES)):
        bs = ps % 2
        wup, wup_tr, wdn, wdn_tr = wupS[bs], wupS_tr[bs], wdnS[bs], wdnS_tr[bs]
        nch = len(PASSES[ps])
        nxt_pcs = weight_pieces(ps + 1) if ps + 1 < len(PASSES) else []
        pend_cast = []
        for tb in range(16):
            hts = [hT_tr[t_] for t_ in range(max(0, tb * 2 - 1), min(NT, tb * 2 + 3))]
            for cst in pend_cast:
                cst("act")
            pend_cast = []
            for _ in range(2):
                if nxt_pcs:
                    iss, cst = nxt_pcs.pop(0)
                    iss()
                    pend_cast.append(cst)
            accs = [(banks[4 + i], bank_tr[4 + i]) for i in range(4)]
            for t2 in range(2):
                tile = tb * 2 + t2
                if ps == 0:
                    dma(xr[t2], x1_d[tile * 128:(tile + 1) * 128, :], reads=[x1_tr], writes=[xr_tr[t2]])
                else:
                    dma(xr[t2], out_d[tile * 128:(tile + 1) * 128, :], reads=[out_tr], writes=[xr_tr[t2]])
            upb = {}

            def up(c):
                bkg, bkg_tr = bank_lo()
                bku, bku_tr = bank_lo()
                for kc in range(KC):
                    mm(bkg[:, 0:258], wup[:, kc, c, :], hT[:, kc, tb * 256:tb * 256 + 258], kc == 0, kc == KC - 1,
                       [wup_tr] + hts, [bkg_tr])
                for kc in range(KC):
                    mm(bku[:, 0:258], wup[:, kc, NCH + c, :], hT[:, kc, tb * 256:tb * 256 + 258], kc == 0, kc == KC - 1,
                       [wup_tr] + hts, [bku_tr])
                upb[c] = (bkg, bkg_tr, bku, bku_tr)

            up(0)
            for c in range(nch):
                ch = PASSES[ps][c]
                if c + 1 < nch:
                    up(c + 1)
                bkg, bkg_tr, bku, bku_tr = upb.pop(c)
                i2 = ci % 2
                ci += 1
                for (dst, dst_tr, src, src_tr, blk) in ((cg[i2], cg_tr[i2], bkg, bkg_tr, ch), (cu[i2], cu_tr[i2], bku, bku_tr, 22 + ch)):
                    act(dst, src[:, 1:257], AF.Copy, [src_tr, fcw_tr], [dst_tr], scale=fcw[:, blk, 1:2])
                    stt(dst, src[:, 0:256], fcw[:, blk, 0:1], dst, ALU.mult, ALU.add, [src_tr, fcw_tr, dst_tr], [dst_tr])
                    stt(dst, src[:, 2:258], fcw[:, blk, 2:3], dst, ALU.mult, ALU.add, [src_tr, fcw_tr, dst_tr], [dst_tr])
                act(cg[i2], cg[i2], AF.Silu, [cg_tr[i2]], [cg_tr[i2]])
                tt("pool", ab_[i2], cg[i2], cu[i2], ALU.mult, [cg_tr[i2], cu_tr[i2]], [ab_tr_[i2]])
                for t2 in range(2):
                    for half in range(2):
                        a_, a_tr = accs[t2 * 2 + half]
                        mm(a_[:, :], ab_[i2][:, t2 * 128:(t2 + 1) * 128], wdn[:, c, half * 512:(half + 1) * 512], c == 0, c == nch - 1,
                           [ab_tr_[i2], wdn_tr], [a_tr])
            for t2 in range(2):
                tile = tb * 2 + t2
                for half in range(2):
                    a_, a_tr = accs[t2 * 2 + half]
                    tt("dve", xr[t2][:, half * 512:(half + 1) * 512], a_[:, :], xr[t2][:, half * 512:(half + 1) * 512], ALU.add,
                       [a_tr, xr_tr[t2]], [xr_tr[t2]])
                dma(out_d[tile * 128:(tile + 1) * 128, :], xr[t2], reads=[xr_tr[t2]], writes=[out_tr], eng="pool")
        for cst in pend_cast:
            cst("act")
        while nxt_pcs:
            iss, cst = nxt_pcs.pop(0)
            iss()
            cst("act")
        mark('Dpass_end')

    P.finish()
    P.emit()
    return nc


def _host_inputs(inputs, b):
    f = np.float32
    w_in = np.ascontiguousarray(inputs["w_in"][0], dtype=f)
    a_cols = w_in[:, C_A:C_A + 16]
    b_cols = w_in[:, C_B:C_B + 16]
    w_ab = np.concatenate([a_cols[:, 0:8], b_cols[:, 0:8], a_cols[:, 8:16], b_cols[:, 8:16]], axis=1)
    m = {
        "x": np.ascontiguousarray(inputs["x"][b], dtype=f),
        "w_in": w_in,
        "w_ab": np.ascontiguousarray(w_ab),
        "norm1_w": np.ascontiguousarray(inputs["norm1_w"][0].reshape(KC, 128).T),
        "norm2_w": np.ascontiguousarray(inputs["norm2_w"][0].reshape(KC, 128).T),
        "dn_conv_w": np.ascontiguousarray(inputs["dn_conv_w"][0].reshape(3, 12, 128).transpose(2, 1, 0)),
        "dn_a_log": np.ascontiguousarray(inputs["dn_a_log"][0].reshape(1, 16)),
        "dn_dt_bias": np.ascontiguousarray(inputs["dn_dt_bias"][0].reshape(1, 16)),
        "dn_out_norm_w": np.ascontiguousarray(inputs["dn_out_norm_w"][0].reshape(1, 64)),
        "attn_q_norm_w": np.ascontiguousarray(np.tile(inputs["attn_q_norm_w"][0], 2).reshape(128, 1)),
        "attn_k_norm_w": np.ascontiguousarray(np.tile(inputs["attn_k_norm_w"][0], 2).reshape(128, 1)),
        "w_dn_out": np.ascontiguousarray(inputs["w_dn_out"][0]),
        "w_attn_out": np.ascontiguousarray(inputs["w_attn_out"][0]),
        "w_o": np.ascontiguousarray(inputs["w_o"][0]),
        "w_ffn_up": np.ascontiguousarray(inputs["w_ffn_up"][0]),
        "ffn_conv_w": np.ascontiguousarray(inputs["ffn_conv_w"][0].reshape(3, 44, 128).transpose(2, 1, 0)),
        "w_ffn_down": np.ascontiguousarray(inputs["w_ffn_down"][0]),
        "cb": _pack_bf16_consts(),
        "amask": _attn_masks(),
    }
    return m


def kernel(**inputs):
    nc = build()
    in_maps = [_host_inputs(inputs, b) for b in range(8)]
    res = run_bass_kernel_spmd(nc, in_maps, core_ids=list(range(8)))
    out = np.stack([np.asarray(res.results[b]["out"], dtype=np.float32) for b in range(8)], axis=0)
    return out
```

```python
import numpy as np
import ml_dtypes
from contextlib import ExitStack
import concourse.bass as bass
import concourse.mybir as mybir
from concourse.bass_utils import run_bass_kernel_spmd

F32 = mybir.dt.float32
BF16 = mybir.dt.bfloat16
AF = mybir.ActivationFunctionType
ALU = mybir.AluOpType
AX = mybir.AxisListType

S = 4096
D = 1024
NT = 32
KC = 8
DIN = 8736
DFF = 2816
EPS = 1e-6
NEG = -30000.0

C_QKV = 0
C_Z = 1536
C_A = 2048
C_B = 2064
C_QB = 2080
C_KB = 2080 + 1536
C_VB = 2080 + 3072
C_GA = 2080 + 4608
C_GB = C_GA + 1024

ATT_GROUPS = ((128, 1), (512, 4), (2048, 16))


class Tr:
    __slots__ = ("lw", "rd")

    def __init__(self):
        self.lw = None
        self.rd = {}


def trs(n):
    return [Tr() for _ in range(n)]


NDS = 24


PHASE_MARKS = []


class Prog:
    ENG = ("pe", "dve", "act", "pool", "sp")

    def __init__(self, nc):
        self.nc = nc
        self.q = {e: [] for e in self.ENG}
        self.cnt = {e: 0 for e in self.ENG}
        self.waited = {e: {} for e in self.ENG}
        self.ndma = 0
        self.dcnt = [0] * NDS

    def _sync(self, eng, reads, writes):
        deps = {}

        def add(k, v):
            if deps.get(k, 0) < v:
                deps[k] = v

        for r in reads:
            if r.lw is not None:
                add(*r.lw)
        for w in writes:
            if w.lw is not None:
                add(*w.lw)
            for k, v in w.rd.items():
                add(k, v)
        wt = self.waited[eng]
        for k, v in deps.items():
            if eng == "pe" and k == "pe":
                continue
            if wt.get(k, 0) < v:
                self.q[eng].append(("w", k, v))
                wt[k] = v

    def op(self, eng, fn, reads=(), writes=()):
        self._sync(eng, reads, writes)
        self.cnt[eng] += 1
        seq = self.cnt[eng]
        self.q[eng].append(("o", fn, eng, 1))
        for r in reads:
            if r.rd.get(eng, 0) < seq:
                r.rd[eng] = seq
        for w in writes:
            w.lw = (eng, seq)
            w.rd = {}

    def dma(self, fn, reads=(), writes=(), eng="sp"):
        self._sync(eng, reads, writes)
        k = self.ndma % NDS
        self.ndma += 1
        key = "d%d" % k
        prev = self.dcnt[k]
        wt = self.waited[eng]
        if prev > 0 and wt.get(key, 0) < prev:
            self.q[eng].append(("w", key, prev))
            wt[key] = prev
        self.dcnt[k] += 16
        val = self.dcnt[k]
        self.q[eng].append(("o", fn, key, 16))
        for r in reads:
            if r.rd.get(key, 0) < val:
                r.rd[key] = val
        for w in writes:
            w.lw = (key, val)
            w.rd = {}

    def finish(self):
        wt = self.waited["sp"]
        for k in range(NDS):
            key = "d%d" % k
            if self.dcnt[k] > 0 and wt.get(key, 0) < self.dcnt[k]:
                self.q["sp"].append(("w", key, self.dcnt[k]))
        for e in ("pe", "dve", "act", "pool"):
            if self.cnt[e] > 0:
                self.q["sp"].append(("w", e, self.cnt[e]))

    def emit(self):
        nc = self.nc
        with ExitStack() as st:
            sems = {}
            for e in ("pe", "dve", "act", "pool"):
                sems[e] = st.enter_context(nc.semaphore("s_" + e))
            for k in range(NDS):
                sems["d%d" % k] = st.enter_context(nc.semaphore("s_d%d" % k))
            block = st.enter_context(nc.Block())

            def run(e, name):
                for it in self.q[name]:
                    if it[0] == "w":
                        e.wait_ge(sems[it[1]], it[2])
                    else:
                        ins = it[1](e)
                        ins.then_inc(sems[it[2]], it[3])

            @block.tensor
            def _(e):
                run(e, "pe")

            @block.vector
            def _(e):
                run(e, "dve")

            @block.scalar
            def _(e):
                run(e, "act")

            @block.gpsimd
            def _(e):
                run(e, "pool")

            @block.sync
            def _(e):
                run(e, "sp")


def _consts():
    t = np.arange(128)
    c = {}
    c["ident"] = np.eye(128, dtype=np.float32)
    c["ones"] = np.ones((128, 128), np.float32)
    bo = np.zeros((128, 128), np.float32)
    bo[:64, :64] = 1.0 / 64
    bo[64:, 64:] = 1.0 / 64
    c["bones"] = bo
    bo1 = np.zeros((128, 128), np.float32)
    bo1[:64, :64] = 1.0
    bo1[64:, 64:] = 1.0
    c["bones1"] = bo1
    T_, I_ = t[:, None], t[None, :]
    L = np.stack([(T_ <= I_), (T_ >= I_)]).astype(np.float32)
    Sm = np.stack([(T_ > I_), (T_ < I_)]).astype(np.float32)
    negS = np.stack([np.where(T_ > I_, 0.0, NEG), np.where(T_ < I_, 0.0, NEG)]).astype(np.float32)
    negST = np.stack([negS[0].T, negS[1].T])
    negI = np.stack([np.where(T_ >= I_, 0.0, NEG), np.where(T_ <= I_, 0.0, NEG)]).astype(np.float32)
    negIT = np.stack([negI[0].T, negI[1].T])
    c["L"] = L
    c["Sm"] = Sm
    c["negS"] = negS
    blk = (T_ // 64 == I_ // 64)
    negST = np.stack([np.where(blk, negST[d], NEG) for d in range(2)])
    c["negST2"] = np.stack([np.stack([negST[d], negST[d]]) for d in range(2)])
    c["off"] = np.stack([((T_ >= 64) & (I_ < 64)), ((T_ < 64) & (I_ >= 64))]).astype(np.float32)
    c["negIT2"] = np.stack([np.stack([negIT[d], negIT[d]]) for d in range(2)])
    return c


def _pack_bf16_consts():
    c = _consts()
    parts = [c["ident"], c["ones"], c["bones"], c["bones1"],
             c["L"][0], c["L"][1], c["Sm"][0], c["Sm"][1],
             c["negS"][0], c["negS"][1],
             c["negST2"][0, 0], c["negST2"][0, 1], c["negST2"][1, 0], c["negST2"][1, 1],
             c["negIT2"][0, 0], c["negIT2"][0, 1], c["negIT2"][1, 0], c["negIT2"][1, 1],
             c["off"][0], c["off"][1]]
    arr = np.concatenate(parts, axis=1)
    return arr.astype(ml_dtypes.bfloat16)


CB_IDENT, CB_ONES, CB_BONES, CB_BONES1, CB_L, CB_SM, CB_NEGS, CB_NEGST, CB_NEGIT = 0, 1, 2, 3, 4, 6, 8, 10, 14
CB_OFF = 18
NCB = 20


def _attn_masks():
    slopes = 2.0 ** (-8.0 * np.arange(1, 25, dtype=np.float32) / 24).reshape(3, 8)
    t = np.arange(128)
    J, I = t[:, None], t[None, :]
    out = np.zeros((3, 8, 2, 128, 128), np.float32)
    for g, (_, dil) in enumerate(ATT_GROUPS):
        for h in range(8):
            for ab, sh in enumerate((64, -64)):
                rel = np.abs(I - J + sh).astype(np.float32)
                out[g, h, ab] = np.where(rel <= 64, np.exp(-slopes[g, h] * rel * dil), 0.0)
    return np.ascontiguousarray(out.transpose(0, 3, 1, 2, 4)).reshape(3, 128, 8 * 2 * 128)


def build(debug=None):
    nc = bass.Bass("TRN2", target_bir_lowering=False)
    P = Prog(nc)
    dram = {}

    def din(name, shape, dt=F32):
        dram[name] = nc.dram_tensor(name, list(shape), dt, kind="ExternalInput").ap()
        return dram[name]

    x_d = din("x", [S, D])
    w_in = din("w_in", [D, DIN])
    w_ab = din("w_ab", [D, 32])
    n1w = din("norm1_w", [128, KC])
    n2w = din("norm2_w", [128, KC])
    dn_cw = din("dn_conv_w", [128, 12, 3])
    dn_alog = din("dn_a_log", [1, 16])
    dn_dtb = din("dn_dt_bias", [1, 16])
    dn_onw = din("dn_out_norm_w", [1, 64])
    aq_w = din("attn_q_norm_w", [128, 1])
    ak_w = din("attn_k_norm_w", [128, 1])
    w_dno = din("w_dn_out", [512, D])
    w_ato = din("w_attn_out", [512, D])
    w_o = din("w_o", [D, D])
    w_up = din("w_ffn_up", [44, 128, KC * 128])
    f_cw = din("ffn_conv_w", [128, 44, 3])
    w_dn = din("w_ffn_down", [DFF, D])
    cb_d = din("cb", [128, NCB * 128], BF16)
    am_d = din("amask", [3, 128, 8 * 2 * 128])
    out_d = nc.dram_tensor("out", [S, D], F32, kind="ExternalOutput").ap()
    oaT_d = nc.dram_tensor("oaT_scr", [512, S], BF16, kind="Internal").ap()
    oB_d = [nc.dram_tensor("oB_scr%d" % g, [S, 8 * 65], BF16, kind="Internal").ap() for g in range(3)]
    oaT_tr = Tr()
    x1_d = nc.dram_tensor("x1_scr", [S, D], F32, kind="Internal").ap()
    dbg_d = None
    if debug is not None:
        dbg_d = nc.dram_tensor("dbg", list(debug["shape"]), debug.get("dt", F32), kind="ExternalOutput").ap()

    ARENA_BYTES = 206 * 1024
    arena = nc.alloc_sbuf_tensor("arena", [128, ARENA_BYTES // 4], F32)

    class Alloc:
        def __init__(self, base=0, dbg=False):
            self.off = base
            self.dbg = dbg

        def get(self, free_shape, dt, parts=128):
            n = int(np.prod(free_shape))
            nbytes = n * (2 if dt == BF16 else 4)
            nbytes = (nbytes + 31) // 32 * 32
            a = arena[0:parts, self.off // 4:(self.off + nbytes) // 4]
            self.off += nbytes
            assert self.off <= ARENA_BYTES - (0 if (debug is None or self.dbg) else 4096), ("SBUF arena overflow", self.off)
            if dt == BF16:
                a = a.bitcast(BF16)
            a = a[:, 0:n]
            if len(free_shape) == 2:
                a = a.rearrange("p (a b) -> p a b", b=free_shape[1])
            elif len(free_shape) == 3:
                a = a.rearrange("p (a b c) -> p a b c", b=free_shape[1], c=free_shape[2])
            elif len(free_shape) == 4:
                a = a.rearrange("p (a b c d) -> p a b c d", b=free_shape[1], c=free_shape[2], d=free_shape[3])
            return a

    pers = Alloc(0)
    hT = pers.get([KC, S + 2], BF16)
    hT_tr = trs(NT)
    cb = pers.get([NCB, 128], BF16)
    cb_tr = Tr()
    vec = pers.get([64], F32)
    vec_tr = Tr()
    PERS_END = pers.off

    banks = [nc.alloc_psum_tensor("bank%d" % i, [128, 512], F32) for i in range(8)]
    bank_tr = trs(8)
    bank_rr = [0]

    def bank():
        for _ in range(8):
            i = bank_rr[0] % 8
            bank_rr[0] += 1
            t_ = bank_tr[i]
            if t_.lw is None or len(t_.rd) > 0:
                return banks[i], t_
        raise RuntimeError("no free PSUM bank")

    ident = cb[:, CB_IDENT, :]
    ones_b = cb[:, CB_ONES, :]

    def mark(tag):
        PHASE_MARKS.append((tag, dict(P.cnt)))

    def barrier_all(tag=None):
        if tag is not None:
            PHASE_MARKS.append((tag, dict(P.cnt)))
        for e in P.ENG:
            wt = P.waited[e]
            for o in ("pe", "dve", "act", "pool"):
                if o == e and e == "pe":
                    continue
                if P.cnt[o] > 0 and wt.get(o, 0) < P.cnt[o]:
                    P.q[e].append(("w", o, P.cnt[o]))
                    wt[o] = P.cnt[o]
            for k in range(NDS):
                key = "d%d" % k
                if P.dcnt[k] > 0 and wt.get(key, 0) < P.dcnt[k]:
                    P.q[e].append(("w", key, P.dcnt[k]))
                    wt[key] = P.dcnt[k]

    P.dma(lambda e: e.dma_start(out=cb.rearrange("p a b -> p (a b)"), in_=cb_d), writes=[cb_tr])
    P.dma(lambda e: e.dma_start(out=vec[:, 0:8], in_=n1w), writes=[vec_tr])
    P.dma(lambda e: e.dma_start(out=vec[:, 8:16], in_=n2w), writes=[vec_tr])
    P.dma(lambda e: e.dma_start(out=vec[:, 16:17], in_=aq_w), writes=[vec_tr])
    P.dma(lambda e: e.dma_start(out=vec[:, 17:18], in_=ak_w), writes=[vec_tr])

    def load_w(dst, dst_tr, src, nkc, ncols, stage, stage_tr, cast_eng="pool"):
        for kc in range(nkc):
            sb, sb_tr = stage[kc % len(stage)], stage_tr[kc % len(stage)]
            P.dma(lambda e, sb=sb, kc=kc: e.dma_start(out=sb[:, 0:ncols], in_=src[kc * 128:(kc + 1) * 128, :]),
                  writes=[sb_tr])
            P.op(cast_eng, lambda e, sb=sb, kc=kc: e.tensor_copy(out=dst[:, kc, 0:ncols], in_=sb[:, 0:ncols]),
                 reads=[sb_tr], writes=[dst_tr])

    def rmsnorm_to_hT(get_x, wcol0, ph):
        pass

    ph = Alloc(PERS_END)
    xs = [ph.get([D], F32) for _ in range(4)]
    xs_tr = trs(4)
    sq = ph.get([D], F32)
    sq_tr = Tr()
    st = [ph.get([8], F32) for _ in range(2)]
    st_tr = trs(2)
    xn = [ph.get([D], BF16) for _ in range(2)]
    xn_tr = trs(2)

    P.op("pool", lambda e: e.memset(hT[:, :, 0:1], 0.0), writes=[hT_tr[0]])
    P.op("pool", lambda e: e.memset(hT[:, :, S + 1:S + 2], 0.0), writes=[hT_tr[NT - 1]])

    def norm_tile(xa, xa_tr, t, i2, wcol0, scr, defer=False):
        sq, sq_tr, st, st_tr, xn, xn_tr = scr
        s_, s_tr = st[i2], st_tr[i2]
        P.op("act", lambda e: e.activation(out=sq, in_=xa, func=AF.Square), reads=[xa_tr], writes=[sq_tr])
        P.op("dve", lambda e: e.tensor_reduce(out=s_[:, 0:1], in_=sq, axis=AX.X, op=ALU.add),
             reads=[sq_tr], writes=[s_tr])
        P.op("act", lambda e: e.activation(out=s_[:, 1:2], in_=s_[:, 0:1], func=AF.Sqrt, bias=EPS, scale=1.0 / D),
             reads=[s_tr], writes=[s_tr])
        P.op("dve", lambda e: e.reciprocal(out=s_[:, 2:3], in_=s_[:, 1:2]), reads=[s_tr], writes=[s_tr])
        P.op("dve", lambda e: e.tensor_scalar(out=xn[i2], in0=xa, scalar1=s_[:, 2:3], scalar2=None, op0=ALU.mult),
             reads=[xa_tr, s_tr], writes=[xn_tr[i2]])
        def part_b():
            for half in range(2):
                bk, bk_tr = bank()
                pb = bk[:, 0:256].bitcast(BF16).rearrange("p (a b) -> p a b", b=128)
                for c in range(4):
                    kc = half * 4 + c
                    P.op("pe", lambda e, c=c, kc=kc, pb=pb: e.transpose(out=pb[:, c, :], in_=xn[i2][:, kc * 128:(kc + 1) * 128],
                                                                       identity=ident),
                         reads=[xn_tr[i2], cb_tr], writes=[bk_tr])
                for c in range(4):
                    kc = half * 4 + c
                    eng = "act" if c % 2 == 0 else "dve"
                    if eng == "act":
                        P.op("act", lambda e, c=c, kc=kc, pb=pb: e.activation(
                            out=hT[:, kc, 1 + t * 128:1 + (t + 1) * 128], in_=pb[:, c, :], func=AF.Copy,
                            scale=vec[:, wcol0 + kc:wcol0 + kc + 1]), reads=[bk_tr, vec_tr], writes=[hT_tr[t]])
                    else:
                        P.op("dve", lambda e, c=c, kc=kc, pb=pb: e.tensor_scalar(
                            out=hT[:, kc, 1 + t * 128:1 + (t + 1) * 128], in0=pb[:, c, :],
                            scalar1=vec[:, wcol0 + kc:wcol0 + kc + 1], scalar2=None, op0=ALU.mult),
                            reads=[bk_tr, vec_tr], writes=[hT_tr[t]])

        if defer:
            return part_b
        part_b()

    pend0 = [None]
    for t in range(NT):
        i2 = t % 2
        i4 = t % 4
        P.dma(lambda e, t=t, i4=i4: e.dma_start(out=xs[i4], in_=x_d[t * 128:(t + 1) * 128, :]), writes=[xs_tr[i4]],
              eng=("sp" if t % 2 == 0 else "pool"))
        nb0 = norm_tile(xs[i4], xs_tr[i4], t, i2, 0, (sq, sq_tr, st, st_tr, xn, xn_tr), defer=True)
        if pend0[0] is not None:
            pend0[0]()
        pend0[0] = nb0

    pend0[0]()
    if debug is not None and debug["what"] == "hT":
        barrier_all()
        stg = ph.get([S], F32)
        stg_tr = Tr()
        for kc in range(KC):
            P.op("dve", lambda e, kc=kc: e.tensor_copy(out=stg, in_=hT[:, kc, 1:S + 1]), reads=hT_tr, writes=[stg_tr])
            P.dma(lambda e, kc=kc: e.dma_start(out=dbg_d[kc * 128:(kc + 1) * 128, :], in_=stg), reads=[stg_tr])
        P.finish()
        P.emit()
        return nc


    def mm(out, lhsT, rhs, start, stop, reads, writes):
        P.op("pe", lambda e: e.matmul(out, lhsT=lhsT, rhs=rhs, start=start, stop=stop), reads, writes)

    def tp(out, in_, reads, writes, idn=None):
        idn = ident if idn is None else idn
        P.op("pe", lambda e: e.transpose(out=out, in_=in_, identity=idn), list(reads) + [cb_tr], writes)

    def act(out, in_, func, reads, writes, bias=None, scale=None):
        kw = {}
        if bias is not None:
            kw["bias"] = bias
        if scale is not None:
            kw["scale"] = scale
        P.op("act", lambda e: e.activation(out=out, in_=in_, func=func, **kw), reads, writes)

    def tt(eng, out, in0, in1, op, reads, writes):
        P.op(eng, lambda e: e.tensor_tensor(out=out, in0=in0, in1=in1, op=op), reads, writes)

    def ts(eng, out, in0, s1, op0, reads, writes, s2=None, op1=None):
        if op1 is None:
            P.op(eng, lambda e: e.tensor_scalar(out=out, in0=in0, scalar1=s1, scalar2=None, op0=op0), reads, writes)
        else:
            P.op(eng, lambda e: e.tensor_scalar(out=out, in0=in0, scalar1=s1, scalar2=s2, op0=op0, op1=op1), reads, writes)

    def stt(out, in0, scalar, in1, op0, op1, reads, writes):
        P.op("dve", lambda e: e.scalar_tensor_tensor(out=out, in0=in0, scalar=scalar, in1=in1, op0=op0, op1=op1),
             reads, writes)

    def cp(eng, out, in_, reads, writes):
        if eng == "act":
            P.op("act", lambda e: e.activation(out=out, in_=in_, func=AF.Copy), reads, writes)
        else:
            P.op(eng, lambda e: e.tensor_copy(out=out, in_=in_), reads, writes)

    def memset(eng, ap, val, writes):
        P.op(eng, lambda e: e.memset(ap, val), (), writes)

    def dma(out, in_, reads=(), writes=(), eng="sp"):
        P.dma(lambda e: e.dma_start(out=out, in_=in_), reads, writes, eng=eng)

    def load_w2(dst, dst_tr, src, nkc, ncols, stage, stage_tr, cast_eng="pool"):
        sv = stage[:, 0:nkc * ncols].rearrange("p (k c) -> p k c", c=ncols)
        dma(sv, src.rearrange("(k p) c -> p k c", p=128), writes=[stage_tr])
        cp(cast_eng, dst, sv, [stage_tr], [dst_tr])

    def dump(ap_list, rows_each):
        da = Alloc(ARENA_BYTES - 4096, dbg=True)
        stgs = [da.get([512], F32), da.get([512], F32)]
        stg_trs = [Tr(), Tr()]
        r0 = 0
        i = 0
        for ap, tr_ in ap_list:
            pp, n = ap.shape[0], ap.shape[1]
            for c in range(0, n, 512):
                w = min(512, n - c)
                cp("dve", stgs[i % 2][0:pp, 0:w], ap[:, c:c + w], [tr_], [stg_trs[i % 2]])
                dma(dbg_d[r0:r0 + pp, c:c + w], stgs[i % 2][0:pp, 0:w], reads=[stg_trs[i % 2]])
                i += 1
            r0 += rows_each
        P.finish()
        P.emit()
        return nc

    barrier_all('p0_end')
    pa = Alloc(PERS_END)
    wstage = pa.get([KC * 128], F32)
    wstage_tr = Tr()
    prm = pa.get([64], F32)
    prm_tr = Tr()
    onw = pa.get([64], F32)
    onw_tr = Tr()
    g32 = pa.get([NT, 16], F32); g32_tr = Tr()
    gbf = pa.get([NT, 16], BF16); gbf_tr = Tr()
    lnb = pa.get([NT, 16], F32); lnb_tr = Tr()
    beta = pa.get([NT, 16], F32); beta_tr = Tr()
    sKbg = pa.get([NT, 16], F32); sKbg_tr = Tr()
    sKd = pa.get([NT, 16], F32); sKd_tr = Tr()
    sQg = pa.get([NT, 16], F32); sQg_tr = Tr()
    dec = pa.get([NT, 16], F32); dec_tr = Tr()

    A1_MARK = pa.off
    wab = pa.get([KC, 32], BF16)
    wab_tr = Tr()
    ab = pa.get([NT, 32], F32)
    ab_tr = Tr()
    gcs = pa.get([NT, 16], F32); gcs_tr = Tr()
    tmpA = pa.get([NT, 16], F32); tmpA_tr = Tr()
    dma(prm[:, 0:16], dn_alog.partition_broadcast(128), writes=[prm_tr])
    dma(prm[:, 16:32], dn_dtb.partition_broadcast(128), writes=[prm_tr])
    dma(onw[:, 0:64], dn_onw.partition_broadcast(128), writes=[onw_tr])
    dma(wstage[:, 0:KC * 32].rearrange("p (k c) -> p k c", c=32), w_ab.rearrange("(k p) c -> p k c", p=128),
        writes=[wstage_tr])
    cp("pool", wab, wstage[:, 0:KC * 32].rearrange("p (k c) -> p k c", c=32), [wstage_tr], [wab_tr])

    bkA, bkA_tr = bank()
    bkB, bkB_tr = bank()
    for n in range(NT):
        bk, bk_tr = (bkA, bkA_tr) if n < 16 else (bkB, bkB_tr)
        for d in range(2):
            tl = n if d == 0 else NT - 1 - n
            o = bk[:, (n % 16) * 32 + d * 16:(n % 16) * 32 + d * 16 + 16]
            for kc in range(KC):
                mm(o, hT[:, kc, 1 + tl * 128:1 + (tl + 1) * 128], wab[:, kc, d * 16:(d + 1) * 16],
                   kc == 0, kc == KC - 1, [hT_tr[tl], wab_tr], [bk_tr])
    cp("act", ab[:, 0:16, :], bkA[:, :].rearrange("p (a b) -> p a b", b=32), [bkA_tr], [ab_tr])
    cp("act", ab[:, 16:32, :], bkB[:, :].rearrange("p (a b) -> p a b", b=32), [bkB_tr], [ab_tr])
    ab4 = ab.rearrange("p n (d k h) -> p n d k h", d=2, k=2)
    a_v = ab4[:, :, :, 0, :]
    b_v = ab4[:, :, :, 1, :]
    g4 = g32.rearrange("p n (d h) -> p n d h", d=2)
    t4 = tmpA.rearrange("p n (d h) -> p n d h", d=2)
    l4 = lnb.rearrange("p n (d h) -> p n d h", d=2)
    act(prm[:, 32:48], prm[:, 0:16], AF.Exp, [prm_tr], [prm_tr])
    ts("dve", prm[:, 32:48], prm[:, 32:48], -1.0, ALU.mult, [prm_tr], [prm_tr])
    tt("dve", t4, a_v, prm[:, 16:32].rearrange("p (d h) -> p d h", d=2).unsqueeze(1).broadcast_to([128, NT, 2, 8]),
       ALU.add, [ab_tr, prm_tr], [tmpA_tr])
    act(tmpA, tmpA, AF.Exp, [tmpA_tr], [tmpA_tr])
    act(tmpA, tmpA, AF.Ln, [tmpA_tr], [tmpA_tr], bias=1.0)
    tt("dve", g32, tmpA, prm[:, 32:48].unsqueeze(1).broadcast_to([128, NT, 16]), ALU.mult, [tmpA_tr, prm_tr], [g32_tr])
    cp("dve", gbf, g32, [g32_tr], [gbf_tr])
    act(l4, b_v, AF.Exp, [ab_tr], [lnb_tr], scale=-1.0)
    act(lnb, lnb, AF.Ln, [lnb_tr], [lnb_tr], bias=1.0)
    act(beta, lnb, AF.Exp, [lnb_tr], [beta_tr], scale=-1.0)
    ts("dve", lnb, lnb, -1.0, ALU.mult, [lnb_tr, beta_tr], [lnb_tr])
    bkC, bkC_tr = bank()
    bkG, bkG_tr = bank()
    gb4 = gbf.rearrange("p n (d h) -> p n d h", d=2)
    c4 = bkC[:, :].rearrange("p (n d h) -> p n d h", d=2, h=8)
    for d in range(2):
        mm(c4[:, :, d, :], cb[:, CB_L + d, :], gb4[:, :, d, :], True, True, [gbf_tr, cb_tr], [bkC_tr])
    mm(bkG[:, :], ones_b, gbf.rearrange("p n c -> p (n c)"), True, True, [gbf_tr, cb_tr], [bkG_tr])
    gcf = gcs.rearrange("p n c -> p (n c)")
    cp("act", gcf, bkC[:, :], [bkC_tr], [gcs_tr])
    act(sQg.rearrange("p n c -> p (n c)"), gcf, AF.Exp, [gcs_tr], [sQg_tr], bias=float(np.log(0.125)))
    act(tmpA.rearrange("p n c -> p (n c)"), gcf, AF.Exp, [gcs_tr], [tmpA_tr])
    tt("dve", sKbg, tmpA, beta, ALU.mult, [tmpA_tr, beta_tr], [sKbg_tr])
    tt("dve", sKd.rearrange("p n c -> p (n c)"), bkG[:, :], gcf, ALU.subtract, [bkG_tr, gcs_tr], [sKd_tr])
    act(sKd, sKd, AF.Exp, [sKd_tr], [sKd_tr])
    act(dec[0:64].rearrange("p n c -> p (n c)"), bkG[0:64, :], AF.Exp, [bkG_tr], [dec_tr])

    if debug is not None and debug["what"] == "scal":
        return dump([(g32.rearrange("p n c -> p (n c)"), g32_tr), (beta.rearrange("p n c -> p (n c)"), beta_tr),
                     (sKbg.rearrange("p n c -> p (n c)"), sKbg_tr), (sKd.rearrange("p n c -> p (n c)"), sKd_tr),
                     (sQg.rearrange("p n c -> p (n c)"), sQg_tr), (dec.rearrange("p n c -> p (n c)"), dec_tr)], 128)

    barrier_all('A1_end')
    pa.off = A1_MARK
    wq = pa.get([KC, 4, 128], BF16); wq_tr = Tr()
    xT = [pa.get([S], BF16) for _ in range(3)]
    xT_tr = [Tr() for _ in range(3)]
    cwt = pa.get([12, 3], F32); cwt_tr = Tr()
    dma(cwt, dn_cw, writes=[cwt_tr])
    Vn = pa.get([4, 64], BF16); Vn_tr = Tr()
    S32 = pa.get([4, 64], F32); S32_tr = Tr()
    Sd = pa.get([4, 64], F32); Sd_tr = Tr()
    Sbf = pa.get([4, 64], BF16); Sbf_tr = Tr()
    RA = pa.off
    pj = Alloc(RA)
    pre = pj.get([S + 2], F32); pre_tr = Tr()
    cv = pj.get([S], F32); cv_tr = Tr()
    sqb = pj.get([S], BF16); sqb_tr = Tr()
    rn = pj.get([512], F32); rn_tr = Tr()
    sc = Alloc(RA)
    Oacc = sc.get([NT, 128], F32); Oacc_tr = Tr()

    def mk_set():
        B = {}
        B["Kbg"] = sc.get([4, 64], BF16); B["Kd_"] = sc.get([4, 64], BF16); B["Qg"] = sc.get([4, 64], BF16); B["Vb"] = sc.get([4, 64], BF16)
        B["tok_tr"] = [Tr() for _ in range(4)]
        B["QgT"] = sc.get([4, 128], BF16); B["QgT_tr"] = Tr()
        for nm_ in ("b1", "b2", "b3", "b4", "b5", "e0", "e1", "e2", "p0", "x0", "x1", "bo"):
            B[nm_] = sc.get([4, 128], BF16); B[nm_ + "_tr"] = Tr()
        return B

    sets = [mk_set() for _ in range(4)]
    a4 = Alloc(RA + NT * 128 * 4)
    szb = a4.get([4, 128], F32); szb_tr = Tr()
    ob = a4.get([4, 128], F32); ob_tr = Tr()
    osq = a4.get([4, 128], F32); osq_tr = Tr()
    ost = a4.get([16], F32); ost_tr = Tr()
    ogb = a4.get([4, 128], BF16); ogb_tr = Tr()
    oaT_stage = a4.get([S], BF16); oaT_stage_tr = Tr()
    pa.off = max(pj.off, sc.off, a4.off)

    identb4 = ident.unsqueeze(1).broadcast_to([128, 4, 128])
    f2 = lambda a: a.rearrange("p u c -> p (u c)")

    for hp in range(0 if (debug is not None and debug.get('skipA')) else 4):
        for j, col in enumerate((512 + hp * 128, hp * 128, 1024 + hp * 128, C_Z + hp * 128)):
            sv = wstage[:, 0:KC * 128].rearrange("p (k c) -> p k c", c=128)
            dma(sv, w_in[:, col:col + 128].rearrange("(k p) c -> p k c", p=128), writes=[wstage_tr])
            cp("act", wq[:, :, j, :], sv, [wstage_tr], [wq_tr])
        memset("pool", pre[:, 0:1], 0.0, [pre_tr])
        memset("pool", pre[:, S + 1:S + 2], 0.0, [pre_tr])
        for j in range(3):
            blk = (1, 0, 2)[j] * 4 + hp
            for tb in range(8):
                bk, bk_tr = bank()
                for kc in range(KC):
                    mm(bk[:, :], wq[:, kc, j, :], hT[:, kc, 1 + tb * 512:1 + (tb + 1) * 512], kc == 0, kc == KC - 1,
                       [wq_tr] + hT_tr[tb * 4:tb * 4 + 4], [bk_tr])
                cp("act", pre[:, 1 + tb * 512:1 + (tb + 1) * 512], bk[:, :], [bk_tr], [pre_tr])
            act(cv, pre[:, 1:S + 1], AF.Copy, [pre_tr, cwt_tr], [cv_tr], scale=cwt[:, blk, 1:2])
            stt(cv, pre[:, 0:S], cwt[:, blk, 0:1], cv, ALU.mult, ALU.add, [pre_tr, cwt_tr, cv_tr], [cv_tr])
            stt(cv, pre[:, 2:S + 2], cwt[:, blk, 2:3], cv, ALU.mult, ALU.add, [pre_tr, cwt_tr, cv_tr], [cv_tr])
            act(cv, cv, AF.Silu, [cv_tr], [cv_tr])
            if j < 2:
                act(sqb, cv, AF.Square, [cv_tr], [sqb_tr])
                for tb in range(8):
                    bk, bk_tr = bank()
                    mm(bk[:, :], cb[:, CB_BONES1, :], sqb[:, tb * 512:(tb + 1) * 512], True, True, [sqb_tr, cb_tr], [bk_tr])
                    act(rn, bk[:, :], AF.Ln, [bk_tr], [rn_tr], bias=EPS)
                    act(rn, rn, AF.Exp, [rn_tr], [rn_tr], scale=-0.5)
                    tt("dve", xT[j][:, tb * 512:(tb + 1) * 512], cv[:, tb * 512:(tb + 1) * 512], rn, ALU.mult,
                       [cv_tr, rn_tr], [xT_tr[j]])
            else:
                cp("dve", xT[2], cv, [cv_tr], [xT_tr[2]])

        if debug is not None and debug["what"] == "qkvT" and hp == debug.get("hp", 0):
            return dump([(xT[0], xT_tr[0]), (xT[1], xT_tr[1]), (xT[2], xT_tr[2])], 128)

        barrier_all('A2_end')
        memset("pool", S32, 0.0, [S32_tr])
        memset("pool", Sbf, 0.0, [Sbf_tr])

        def step_setup(n, B):
            tl = (n, NT - 1 - n)
            c0 = (tl[0] * 128, tl[1] * 128)
            hs = slice(hp * 2, hp * 2 + 2)
            tok_tr = B["tok_tr"]
            Kbg, Kd_, Qg, Vb = B["Kbg"], B["Kd_"], B["Qg"], B["Vb"]

            def scal(arr):
                return arr.rearrange("p n (d h) -> p n d h", d=2)[:, n, :, hs]

            f2 = lambda a: a.rearrange("p u c -> p (u c)")
            v4 = lambda a: a.rearrange("p (d h) c -> p d h c", d=2)
            u3 = lambda bk_: bk_[:, :].rearrange("p (u c) -> p u c", c=128)

            def m2(idx):
                return cb[:, idx:idx + 2, :].unsqueeze(2).broadcast_to([128, 2, 2, 128])

            gsc = scal(g32).unsqueeze(3).broadcast_to([128, 2, 2, 128])
            lsc = scal(lnb).unsqueeze(3).broadcast_to([128, 2, 2, 128])
            b1, b1t = B["b1"], B["b1_tr"]
            b2, b2t = B["b2"], B["b2_tr"]
            b3, b3t = B["b3"], B["b3_tr"]
            b4, b4t = B["b4"], B["b4_tr"]
            b5, b5t = B["b5"], B["b5_tr"]
            e0, e0t = B["e0"], B["e0_tr"]
            e1, e1t = B["e1"], B["e1_tr"]
            e2, e2t = B["e2"], B["e2_tr"]
            p0, p0t = B["p0"], B["p0_tr"]
            x0, x0t = B["x0"], B["x0_tr"]
            x1_, x1t = B["x1"], B["x1_tr"]
            bo, bot = B["bo"], B["bo_tr"]
            QgT, QgT_tr = B["QgT"], B["QgT_tr"]
            tt("pool", v4(b1), gsc, m2(CB_SM), ALU.mult, [g32_tr, cb_tr], [b1t])
            tt("pool", v4(b2), gsc, m2(CB_L), ALU.mult, [g32_tr, cb_tr], [b2t])
            tt("pool", v4(b3), lsc, m2(CB_NEGS), ALU.add, [lnb_tr, cb_tr], [b3t])
            tt("pool", v4(b4), lsc, ident.unsqueeze(1).unsqueeze(1).broadcast_to([128, 2, 2, 128]), ALU.mult,
               [lnb_tr, cb_tr], [b4t])
            hm = cb[:, CB_BONES1, 0:128:64]
            for d in range(2):
                tt("pool", b5[:, d * 2:d * 2 + 2, :], xT[0][:, c0[d]:c0[d] + 128].unsqueeze(1).broadcast_to([128, 2, 128]),
                   hm.unsqueeze(2).broadcast_to([128, 2, 128]), ALU.mult, [xT_tr[0], cb_tr], [b5t])
            bk, bk_tr = bank()
            pbt = bk[:, 0:384].bitcast(BF16).rearrange("p (a b) -> p a b", b=128)
            for j in range(3):
                for d in range(2):
                    tp(pbt[:, j * 2 + d, :], xT[j][:, c0[d]:c0[d] + 128], [xT_tr[j]], [bk_tr])
            yield
            srcK = pbt[:, 0:2, :].rearrange("p d (h c) -> p d h c", h=2)
            srcQ = pbt[:, 2:4, :].rearrange("p d (h c) -> p d h c", h=2)
            srcV = pbt[:, 4:6, :].rearrange("p d (h c) -> p d h c", h=2)
            for dst, src, sarr, sarr_tr, k_ in ((Qg, srcQ, sQg, sQg_tr, 2), (Kbg, srcK, sKbg, sKbg_tr, 0), (Kd_, srcK, sKd, sKd_tr, 1),
                                                 (Vb, srcV, beta, beta_tr, 3)):
                tt("dve", dst.rearrange("p (d h) c -> p d h c", d=2), src,
                   scal(sarr).unsqueeze(3).broadcast_to([128, 2, 2, 64]), ALU.mult, [bk_tr, sarr_tr], [tok_tr[k_]])
            bk, bk_tr = bank()
            pq = bk[0:64, 0:256].bitcast(BF16).rearrange("p (a b) -> p a b", b=128)
            for u in range(4):
                tp(pq[:, u, :], Qg[:, u, :], [tok_tr[2]], [bk_tr])
            yield
            cp("act", QgT[0:64], pq, [bk_tr], [QgT_tr])
            bk, bk_tr = bank()
            for d in range(2):
                us = slice(d * 2, d * 2 + 2)
                o = bk[:, d * 256:(d + 1) * 256]
                mm(o, cb[:, CB_L + d, :], b1[:, us, :], True, False, [b1t, cb_tr], [bk_tr])
                mm(o, ident, b3[:, us, :], False, True, [b3t, cb_tr], [bk_tr])
            yield
            act(f2(e0), bk[:, :], AF.Exp, [bk_tr], [e0t])
            bk, bk_tr = bank()
            for d in range(2):
                us = slice(d * 2, d * 2 + 2)
                o = bk[:, d * 256:(d + 1) * 256]
                mm(o, cb[:, CB_SM + d, :], b2[:, us, :], True, False, [b2t, cb_tr], [bk_tr])
                mm(o, ones_b, b4[:, us, :], False, False, [b4t, cb_tr], [bk_tr])
                mm(o, ident, cb[:, CB_NEGST + 2 * d:CB_NEGST + 2 * d + 2, :], False, True, [cb_tr], [bk_tr])
            yield
            act(f2(e1), bk[:, :], AF.Exp, [bk_tr], [e1t])
            bk, bk_tr = bank()
            for d in range(2):
                us = slice(d * 2, d * 2 + 2)
                o = bk[:, d * 256:(d + 1) * 256]
                mm(o, cb[:, CB_SM + d, :], b2[:, us, :], True, False, [b2t, cb_tr], [bk_tr])
                mm(o, ident, cb[:, CB_NEGIT + 2 * d:CB_NEGIT + 2 * d + 2, :], False, True, [cb_tr], [bk_tr])
            yield
            act(f2(e2), bk[:, :], AF.Exp, [bk_tr], [e2t])
            bk, bk_tr = bank()
            for d in range(2):
                for h in range(2):
                    u = d * 2 + h
                    mm(u3(bk)[:, u, :], b5[:, u, :], xT[0][:, c0[d]:c0[d] + 128], True, True, [xT_tr[0], b5t], [bk_tr])
            yield
            stt(f2(e0), bk[:, :], -1.0, f2(e0), ALU.mult, ALU.mult, [bk_tr, e0t], [e0t])
            stt(f2(e1), bk[:, :], -1.0, f2(e1), ALU.mult, ALU.mult, [bk_tr, e1t], [e1t])
            bk, bk_tr = bank()
            for d in range(2):
                for h in range(2):
                    u = d * 2 + h
                    mm(u3(bk)[:, u, :], b5[:, u, :], xT[1][:, c0[d]:c0[d] + 128], True, True, [xT_tr[1], b5t], [bk_tr])
            tt("pool", p0, e0, cb[:, CB_BONES1, :].unsqueeze(1).broadcast_to([128, 4, 128]), ALU.mult, [e0t, cb_tr], [p0t])
            tt("pool", v4(bo), v4(e0), m2(CB_OFF), ALU.mult, [e0t, cb_tr], [bot])
            tt("pool", x0, e1, identb4, ALU.add, [e1t, cb_tr], [x0t])
            yield
            stt(f2(e2), bk[:, :], 0.125, f2(e2), ALU.mult, ALU.mult, [bk_tr, e2t], [e2t])
            Pp, Ppt = [p0, b4], [p0t, b4t]
            PT, PTt = [e1, b5], [e1t, b5t]
            XT, XTt = [x0, x1_], [x0t, x1t]
            cur = 0
            for k in range(5):
                nxt = 1 - cur
                bk, bk_tr = bank()
                for u in range(4):
                    mm(u3(bk)[:, u, :], PT[cur][:, u, :], Pp[cur][:, u, :], True, True, [PTt[cur], Ppt[cur]], [bk_tr])
                yield
                cp("act", f2(Pp[nxt]), bk[:, :], [bk_tr], [Ppt[nxt]])
                if k < 4:
                    bk, bk_tr = bank()
                    for u in range(4):
                        mm(u3(bk)[:, u, :], Pp[cur][:, u, :], PT[cur][:, u, :], True, True, [PTt[cur], Ppt[cur]], [bk_tr])
                    yield
                    cp("act" if k % 2 == 0 else "dve", f2(PT[nxt]), bk[:, :], [bk_tr], [PTt[nxt]])
                bk, bk_tr = bank()
                for u in range(4):
                    mm(u3(bk)[:, u, :], Pp[nxt][:, u, :], XT[cur][:, u, :], True, True, [Ppt[nxt], XTt[cur]], [bk_tr])
                yield
                tt("dve", f2(XT[nxt]), bk[:, :], f2(XT[cur]), ALU.add, [bk_tr, XTt[cur]], [XTt[nxt]])
                cur = nxt
            nxt = 1 - cur
            bk, bk_tr = bank()
            for u in range(4):
                mm(u3(bk)[:, u, :], bo[:, u, :], XT[cur][:, u, :], True, True, [bot, XTt[cur]], [bk_tr])
            yield
            cp("act", f2(b1), bk[:, :], [bk_tr], [b1t])
            bk, bk_tr = bank()
            pxd = bk[:, 0:256].bitcast(BF16).rearrange("p (a b) -> p a b", b=128)
            for u in range(4):
                tp(pxd[:, u, :], XT[cur][:, u, :], [XTt[cur]], [bk_tr])
            yield
            cp("act", b2, pxd, [bk_tr], [b2t])
            bk, bk_tr = bank()
            for u in range(4):
                mm(u3(bk)[:, u, :], b2[:, u, :], b1[:, u, :], True, True, [b2t, b1t], [bk_tr])
            yield
            tt("dve", f2(XT[nxt]), bk[:, :], f2(XT[cur]), ALU.add, [bk_tr, XTt[cur]], [XTt[nxt]])
            TT, TT_tr = XT[nxt], XTt[nxt]
            bk, bk_tr = bank()
            uu = bk[:, 0:256].rearrange("p (u c) -> p u c", c=64)
            for u in range(4):
                mm(uu[:, u, :], TT[:, u, :], Vb[:, u, :], True, True, [TT_tr, tok_tr[3]], [bk_tr])
            yield
            U_sb = e0.rearrange("p u c -> p (u c)").bitcast(F32).rearrange("p (u c) -> p u c", c=64)
            cp("act", U_sb, uu, [bk_tr], [e0t])
            bk, bk_tr = bank()
            w3 = bk[0:64, :].rearrange("p (u c) -> p u c", c=128)
            for u in range(4):
                mm(w3[:, u, :], Kbg[:, u, :], TT[:, u, :], True, True, [TT_tr, tok_tr[0]], [bk_tr])
            yield
            cp("dve", b3[0:64], w3, [bk_tr], [b3t])
            B["r"] = dict(Kd_=Kd_, Kd_tr=tok_tr[1], QgT=QgT, QgT_tr=QgT_tr, QKm=e2, QKm_tr=e2t, U_sb=U_sb, U_tr=e0t, WT=b3, WT_tr=b3t)

        def step_recur(n, B):
            r_ = B["r"]
            Kd_, Kd_tr, QgT, QgT_tr, QKm, QKm_tr, U_sb, U_tr, WT, WT_tr = (r_[k_] for k_ in (
                "Kd_", "Kd_tr", "QgT", "QgT_tr", "QKm", "QKm_tr", "U_sb", "U_tr", "WT", "WT_tr"))
            tl = (n, NT - 1 - n)
            hs = slice(hp * 2, hp * 2 + 2)
            bk1, bk1_tr = bank()
            b13 = bk1[:, 0:256].rearrange("p (u c) -> p u c", c=64)
            for u in range(4):
                mm(b13[:, u, :], WT[0:64, u, :], Sbf[0:64, u, :], True, True, [WT_tr, Sbf_tr], [bk1_tr])
            tt("dve", Vn, U_sb, b13, ALU.subtract, [U_tr, bk1_tr], [Vn_tr])
            bkO, bkO_tr = bank()
            o3 = bkO[:, 0:256].rearrange("p (u c) -> p u c", c=64)
            for u in range(4):
                mm(o3[:, u, :], QgT[0:64, u, :], Sbf[0:64, u, :], True, False, [QgT_tr, Sbf_tr], [bkO_tr])
                mm(o3[:, u, :], QKm[:, u, :], Vn[:, u, :], False, True, [QKm_tr, Vn_tr], [bkO_tr])
            for d in range(2):
                if n < NT // 2:
                    cp("act", Oacc[:, tl[d], :], bkO[:, d * 128:(d + 1) * 128], [bkO_tr], [Oacc_tr])
                else:
                    tt("dve", Oacc[:, tl[d], :], bkO[:, d * 128:(d + 1) * 128], Oacc[:, tl[d], :], ALU.add, [bkO_tr, Oacc_tr], [Oacc_tr])
            bkS, bkS_tr = bank()
            s3 = bkS[0:64, 0:256].rearrange("p (u c) -> p u c", c=64)
            for u in range(4):
                mm(s3[:, u, :], Kd_[:, u, :], Vn[:, u, :], True, True, [Kd_tr, Vn_tr], [bkS_tr])
            dsc = dec.rearrange("p n (d h) -> p n d h", d=2)[0:64, n, :, hs].unsqueeze(3).broadcast_to([64, 2, 2, 64])
            tt("pool", Sd[0:64].rearrange("p (d h) c -> p d h c", d=2), S32[0:64].rearrange("p (d h) c -> p d h c", d=2),
               dsc, ALU.mult, [S32_tr, dec_tr], [Sd_tr])
            tt("dve", S32[0:64], Sd[0:64], s3, ALU.add, [Sd_tr, bkS_tr], [S32_tr])
            cp("act", Sbf[0:64], S32[0:64], [S32_tr], [Sbf_tr])

        NSET = len(sets)
        STAG = 7
        nst = NT if debug is None else debug.get('nsteps', NT)
        active = []
        free_sets = list(range(NSET))
        next_n, tick, last_done = 0, 0, -1
        while next_n < nst or active:
            if next_n < nst and free_sets and (tick % STAG == 0 or not active):
                si = free_sets.pop(0)
                active.append([step_setup(next_n, sets[si]), next_n, si])
                next_n += 1
            for item in list(active):
                try:
                    next(item[0])
                except StopIteration:
                    active.remove(item)
                    assert item[1] == last_done + 1
                    last_done = item[1]
                    step_recur(item[1], sets[item[2]])
                    free_sets.append(item[2])
            tick += 1

        if debug is not None and debug["what"] == "oacc" and hp == debug.get("hp", 0):
            return dump([(Oacc.rearrange("p n c -> p (n c)"), Oacc_tr)], 128)

        barrier_all('scan_end')
        for gq in range(8):
            bk, bk_tr = bank()
            z3 = bk[:, :].rearrange("p (t c) -> p t c", c=128)
            for t_ in range(4):
                tl_ = gq * 4 + t_
                for kc in range(KC):
                    mm(z3[:, t_, :], hT[:, kc, 1 + tl_ * 128:1 + (tl_ + 1) * 128], wq[:, kc, 3, :], kc == 0, kc == KC - 1,
                       [hT_tr[tl_], wq_tr], [bk_tr])
            act(f2(szb), bk[:, :], AF.Silu, [bk_tr], [szb_tr])
            cp("pool", ob, Oacc[:, gq * 4:gq * 4 + 4, :], [Oacc_tr], [ob_tr])
            tt("pool", osq, ob, ob, ALU.mult, [ob_tr], [osq_tr])
            P.op("dve", lambda e: e.tensor_reduce(out=ost[:, 0:8], in_=f2(osq).rearrange("p (a c) -> p a c", c=64),
                                                  axis=AX.X, op=ALU.add), [osq_tr], [ost_tr])
            act(ost[:, 0:8], ost[:, 0:8], AF.Sqrt, [ost_tr], [ost_tr], bias=EPS, scale=1.0 / 64)
            P.op("dve", lambda e: e.reciprocal(out=ost[:, 8:16], in_=ost[:, 0:8]), [ost_tr], [ost_tr])
            o4 = f2(ob).rearrange("p (a c) -> p a c", c=64)
            tt("dve", o4, o4, ost[:, 8:16].unsqueeze(2).broadcast_to([128, 8, 64]), ALU.mult, [ob_tr, ost_tr], [ob_tr])
            tt("pool", o4, o4, onw[:, 0:64].unsqueeze(1).broadcast_to([128, 8, 64]), ALU.mult, [ob_tr, onw_tr], [ob_tr])
            tt("dve", ogb, ob, szb, ALU.mult, [ob_tr, szb_tr], [ogb_tr])
            bk, bk_tr = bank()
            pt4 = bk[:, 0:256].bitcast(BF16).rearrange("p (a b) -> p a b", b=128)
            for t_ in range(4):
                tp(pt4[:, t_, :], ogb[:, t_, :], [ogb_tr], [bk_tr])
            cp("act", oaT_stage[:, gq * 512:(gq + 1) * 512], bk[:, 0:256].bitcast(BF16), [bk_tr], [oaT_stage_tr])
        dma(oaT_d[hp * 128:(hp + 1) * 128, :], oaT_stage, reads=[oaT_stage_tr], writes=[oaT_tr])
        barrier_all('A4_end')

    if debug is not None and debug["what"] == "oaT":
        stg2 = Alloc(PERS_END).get([S], BF16)
        stg2_tr = Tr()
        out_list = []
        for r in range(4):
            dma(stg2, oaT_d[r * 128:(r + 1) * 128, :], reads=[oaT_tr], writes=[stg2_tr])
            stg = Alloc(PERS_END + 16384).get([S], F32)
            stg_tr = Tr()
            cp("dve", stg, stg2, [stg2_tr], [stg_tr])
            dma(dbg_d[r * 128:(r + 1) * 128, :], stg, reads=[stg_tr])
            barrier_all()
        P.finish()
        P.emit()
        return nc

    PAD = 64
    pbl = Alloc(PERS_END)
    wstBs = [pbl.get([KC * 256], F32) for _ in range(2)]; wstBs_tr = [Tr(), Tr()]
    wBs = [[pbl.get([KC, 256], BF16) for _ in range(3)] for _ in range(2)]
    wBs_tr = [[Tr(), Tr(), Tr()] for _ in range(2)]
    ws_i = [0]
    qTg = pbl.get([2, PAD + S + PAD], BF16); qTg_tr = Tr()
    kTg = pbl.get([2, PAD + S + PAD], BF16); kTg_tr = Tr()
    Vaug = pbl.get([NT, 4, 65], BF16); Vaug_tr = Tr()
    amks = [pbl.get([4, 2, 128], F32) for _ in range(2)]; amks_tr = [Tr(), Tr()]
    sqBs = [pbl.get([512], BF16) for _ in range(4)]; sqBs_tr = trs(4)
    rnBs = [pbl.get([512], F32) for _ in range(4)]; rnBs_tr = trs(4)
    nb_i = [0]
    exB = [pbl.get([512], F32) for _ in range(4)]; exB_tr = trs(4)
    ptB = [[[pbl.get([4, 128], BF16) for _ in range(2)] for _ in range(4)] for _ in range(2)]
    ptB_tr = [[[Tr(), Tr()] for _ in range(4)] for _ in range(2)]
    ostg = [pbl.get([4, 65], BF16) for _ in range(2)]; ostg_tr = [Tr(), Tr()]
    oB_tr = [Tr() for _ in range(3)]

    sc_banks = {0: (0,), 1: (1,)}
    sc_rr = {0: 0, 1: 0}
    wk_banks = (2, 3, 4, 5, 6, 7)
    wk_rr = [0]

    def bank_sc(par):
        i = sc_banks[par][sc_rr[par] % len(sc_banks[par])]
        sc_rr[par] += 1
        return banks[i], bank_tr[i]

    def bank_wk():
        i = wk_banks[wk_rr[0] % len(wk_banks)]
        wk_rr[0] += 1
        return banks[i], bank_tr[i]

    memset("pool", qTg, 0.0, [qTg_tr])
    memset("pool", kTg, 0.0, [kTg_tr])
    memset("pool", Vaug, 1.0, [Vaug_tr])
    ex_i = [0]
    os_i = [0]
    def load_attn_w(g_, hh_):
        it2 = (g_ * 2 + hh_) % 2
        for wi, cbase in enumerate((C_QB, C_KB, C_VB)):
            col = cbase + g_ * 512 + hh_ * 256
            wstB, wstB_tr = wstBs[ws_i[0] % 2], wstBs_tr[ws_i[0] % 2]
            ws_i[0] += 1
            sv = wstB[:, 0:KC * 256].rearrange("p (k c) -> p k c", c=256)
            dma(sv, w_in[:, col:col + 256].rearrange("(k p) c -> p k c", p=128), writes=[wstB_tr])
            cp("pool", wBs[it2][wi], sv, [wstB_tr], [wBs_tr[it2][wi]])
        dma(amks[it2].rearrange("p h a c -> p (h a c)"), am_d[g_, :, hh_ * 1024:(hh_ + 1) * 1024], writes=[amks_tr[it2]])

    for g, (_, dil) in enumerate(ATT_GROUPS):
        L = S // dil
        NB = min(512, L)
        for hh in range(2):
            it_ = (g * 2 + hh) % 2
            wqB, wkB, wvB = wBs[it_]
            wB_tr = wBs_tr[it_]
            amk, amk_tr = amks[it_], amks_tr[it_]
            if g == 0 and hh == 0:
                load_attn_w(0, 0)
            units = [(which, hp2, un) for which in range(2) for hp2 in range(2) for un in range(S // 512)]
            qk_info = ((qTg, qTg_tr, wqB, 16), (kTg, kTg_tr, wkB, 17))
            ust = {}

            def qk_proj(ui):
                which, hp2, un = units[ui]
                dst, dst_tr, wt_, wcol = qk_info[which]
                bk, bk_tr = bank_wk()
                for kc in range(KC):
                    mm(bk[:, :], wt_[:, kc, hp2 * 128:(hp2 + 1) * 128], hT[:, kc, 1 + un * 512:1 + (un + 1) * 512], kc == 0,
                       kc == KC - 1, [wB_tr[which]] + hT_tr[un * 4:un * 4 + 4], [bk_tr])
                i4 = nb_i[0] % 4
                nb_i[0] += 1
                act(sqBs[i4], bk[:, :], AF.Square, [bk_tr], [sqBs_tr[i4]])
                ust[ui] = (bk, bk_tr, i4)

            def qk_norm(ui):
                which, hp2, un = units[ui]
                dst, dst_tr, wt_, wcol = qk_info[which]
                bk, bk_tr, i4 = ust.pop(ui)
                bk2, bk2_tr = bank_wk()
                mm(bk2[:, :], cb[:, CB_BONES, :], sqBs[i4], True, True, [sqBs_tr[i4], cb_tr], [bk2_tr])
                act(rnBs[i4], bk2[:, :], AF.Ln, [bk2_tr], [rnBs_tr[i4]], bias=EPS)
                act(rnBs[i4], rnBs[i4], AF.Exp, [rnBs_tr[i4]], [rnBs_tr[i4]], scale=-0.5)
                mb = 512 // dil
                m_base = (un * 512) // dil
                o_ap = dst[:, hp2, PAD:PAD + S].rearrange("p (r l) -> p r l", l=L)[:, :, m_base:m_base + mb]
                i0_ap = bk[:, :].rearrange("p (m r) -> p r m", r=dil)
                i1_ap = rnBs[i4].rearrange("p (m r) -> p r m", r=dil)
                stt(o_ap, i0_ap, vec[:, wcol:wcol + 1], i1_ap, ALU.mult, ALU.mult, [bk_tr, vec_tr, rnBs_tr[i4]], [dst_tr])

            qk_proj(0)
            for ui in range(len(units)):
                if ui + 1 < len(units):
                    qk_proj(ui + 1)
                qk_norm(ui)
            if debug is not None and debug['what'] == 'qkB' and g == debug.get('g', 0) and hh == 0:
                return dump([(qTg[:, 0, PAD:PAD + S], qTg_tr), (kTg[:, 0, PAD:PAD + S], kTg_tr)], 128)
            mark('B%d%d_qk' % (g, hh))
            for pt_ in range(NT):
                p0 = pt_ * 128
                r, m0 = p0 // L, p0 % L
                c_lo = 1 + r + dil * m0
                c_hi = c_lo + dil * 127 + 1
                tiles = sorted(set(range((c_lo - 1) // 128, (c_hi - 2) // 128 + 1)))
                bk, bk_tr = bank_wk()
                for kc in range(KC):
                    mm(bk[:, 0:256], hT[:, kc, c_lo:c_hi:dil], wvB[:, kc, :], kc == 0, kc == KC - 1,
                       [wB_tr[2]] + [hT_tr[t_] for t_ in tiles], [bk_tr])
                cp("act", Vaug[:, pt_, :, 0:64], bk[:, 0:256].rearrange("p (h c) -> p h c", c=64), [bk_tr], [Vaug_tr])
            mark('B%d%d_v' % (g, hh))
            nxt_it = g * 2 + hh + 1
            if nxt_it < 6:
                load_attn_w(nxt_it // 2, nxt_it % 2)
            ntl = L // 128
            qblocks = [(r, c) for r in range(dil) for c in range(-1, ntl)]
            def scores(batch, bp):
                nq = len(batch)
                for hp2, ab, par in [(a_, b_, c_) for a_ in range(2) for b_ in range(2) for c_ in range(2)]:
                    h4 = hp2 * 2 + par
                    rows = slice(par * 64, par * 64 + 64)
                    if True:
                        bk, bk_tr = bank_sc(par)
                        s3_ = bk[:, :].rearrange("p (q c) -> p q c", c=128)
                        for qi, (r, c) in enumerate(batch):
                            kt = c + ab
                            if kt < 0 or kt >= ntl:
                                continue
                            kcol = PAD + r * L + kt * 128
                            qcol = PAD + r * L + 64 + 128 * c
                            mm(s3_[:, qi, :], kTg[rows, hp2, kcol:kcol + 128], qTg[rows, hp2, qcol:qcol + 128], True, True,
                               [kTg_tr, qTg_tr], [bk_tr])
                        ei = ex_i[0] % 4
                        ex_i[0] += 1
                        act(exB[ei][:, 0:nq * 128], bk[:, 0:nq * 128], AF.Exp, [bk_tr], [exB_tr[ei]], scale=0.125)
                        tt("dve", ptB[bp][h4][ab][:, 0:nq, :], exB[ei][:, 0:nq * 128].rearrange("p (q c) -> p q c", c=128),
                           amk[:, h4, ab, :].unsqueeze(1).broadcast_to([128, nq, 128]), ALU.mult, [exB_tr[ei], amk_tr],
                           [ptB_tr[bp][h4][ab]])

            def pv(batch, bp):
                for qi, (r, c) in enumerate(batch):
                    bk, bk_tr = bank_wk()
                    o3_ = bk[:, 0:260].rearrange("p (h c) -> p h c", c=65)
                    va, vb_ = (c >= 0), (c + 1 < ntl)
                    for h4 in range(4):
                        if va:
                            mm(o3_[:, h4, :], ptB[bp][h4][0][:, qi, :], Vaug[:, (r * L) // 128 + c, h4, :], True, not vb_,
                               [ptB_tr[bp][h4][0], Vaug_tr], [bk_tr])
                        if vb_:
                            mm(o3_[:, h4, :], ptB[bp][h4][1][:, qi, :], Vaug[:, (r * L) // 128 + c + 1, h4, :], not va, True,
                               [ptB_tr[bp][h4][1], Vaug_tr], [bk_tr])
                    oi = os_i[0] % 2
                    os_i[0] += 1
                    cp("act", ostg[oi], o3_, [bk_tr], [ostg_tr[oi]])
                    plo, phi = (64 if c < 0 else 0), (64 if c + 1 >= ntl else 128)
                    m_lo = 64 + 128 * c + plo
                    oview = oB_d[g].rearrange("(m d) c -> d m c", d=dil)
                    dma(oview[r, m_lo:m_lo + (phi - plo), hh * 260:(hh + 1) * 260],
                        ostg[oi][plo:phi].rearrange("p h c -> p (h c)"), reads=[ostg_tr[oi]], writes=[oB_tr[g]])

            prev = None
            for bi, b0 in enumerate(range(0, len(qblocks), 4)):
                batch = qblocks[b0:b0 + 4]
                scores(batch, bi % 2)
                if prev is not None:
                    pv(*prev)
                prev = (batch, bi % 2)
            pv(*prev)
            mark('B%d%d_att' % (g, hh))
    barrier_all('B_end')

    if debug is not None and debug["what"] == "oB":
        da_ = Alloc(PERS_END)
        stg = da_.get([520], F32); stg_tr = Tr()
        for g_ in range(3):
            for t_ in range(NT):
                dma(stg, oB_d[g_][t_ * 128:(t_ + 1) * 128, :], reads=[oB_tr[g_]], writes=[stg_tr])
                dma(dbg_d[g_ * S + t_ * 128:g_ * S + (t_ + 1) * 128, :], stg, reads=[stg_tr])
        P.finish()
        P.emit()
        return nc

    pc = Alloc(PERS_END)
    STG = 3072
    wst = pc.get([STG], F32); wst_tr = Tr()
    Wga = pc.get([KC, 1024], BF16); Wgb = pc.get([KC, 1024], BF16); Wo_ = pc.get([KC, 1024], BF16)
    Wdno = pc.get([4, 1024], BF16); Wato = pc.get([4, 1024], BF16)
    Wc_tr = [Tr() for _ in range(5)]

    lb_i = [0]

    def load_big(dst, dst_tr, src, nkc, ncols, stage, stage_tr, cast_eng=None):
        cw = 512 if ncols >= 512 else ncols
        kstep = max(1, STG // cw)
        for c0_ in range(0, ncols, cw):
            w_ = min(cw, ncols - c0_)
            for k0 in range(0, nkc, kstep):
                nk = min(kstep, nkc - k0)
                sv = stage[:, 0:nk * w_].rearrange("p (k c) -> p k c", c=w_)
                dma(sv, src[k0 * 128:(k0 + nk) * 128, c0_:c0_ + w_].rearrange("(k p) c -> p k c", p=128), writes=[stage_tr])
                lb_i[0] += 1
                cp(cast_eng or ("act" if lb_i[0] % 2 else "dve"), dst[:, k0:k0 + nk, c0_:c0_ + w_], sv, [stage_tr], [dst_tr])

    Wc2_tr = [[Tr(), Tr()] for _ in range(5)]
    wsth = [wst[:, 0:STG // 2], wst[:, STG // 2:STG]]
    wsth_tr = [Tr(), Tr()]
    wc_i = [0]

    def load_chunk(dst, wi, src, nkc, ch_):
        c0_ = ch_ * 512
        for k0 in range(0, nkc, 3):
            nk = min(3, nkc - k0)
            hi = wc_i[0] % 2
            wc_i[0] += 1
            sv = wsth[hi][:, 0:nk * 512].rearrange("p (k c) -> p k c", c=512)
            dma(sv, src[k0 * 128:(k0 + nk) * 128, c0_:c0_ + 512].rearrange("(k p) c -> p k c", p=128), writes=[wsth_tr[hi]])
            cp("act" if wc_i[0] % 2 else "dve", dst[:, k0:k0 + nk, c0_:c0_ + 512], sv, [wsth_tr[hi]], [Wc2_tr[wi][ch_]])

    for ch_ in range(2):
        load_chunk(Wdno, 3, w_dno, 4, ch_)
        load_chunk(Wato, 4, w_ato, 4, ch_)
        load_chunk(Wga, 0, w_in[:, C_GA:C_GA + 1024], KC, ch_)
        load_chunk(Wgb, 1, w_in[:, C_GB:C_GB + 1024], KC, ch_)
        if ch_ == 0:
            load_chunk(Wo_, 2, w_o, KC, 0)
    load_chunk(Wo_, 2, w_o, KC, 1)

    oBt = [pc.get([8, 65], BF16) for _ in range(3)]; oBt_tr = [Tr() for _ in range(3)]
    osum = pc.get([8, 65], F32); osum_tr = Tr()
    rden = pc.get([8], F32); rden_tr = Tr()
    obn = pc.get([8, 64], BF16); obn_tr = Tr()
    obTs = [pc.get([4, 512], BF16) for _ in range(2)]; obTs_tr = [Tr(), Tr()]
    oaTbs = [pc.get([4, 512], BF16) for _ in range(2)]; oaTbs_tr = [Tr(), Tr()]
    sga = pc.get([512], BF16); sga_tr = Tr()
    sgb = pc.get([512], BF16); sgb_tr = Tr()
    mixT = pc.get([8, 512], BF16); mixT_tr = Tr()
    xsC = [pc.get([D], F32) for _ in range(2)]; xsC_tr = [Tr(), Tr()]
    x1t = [pc.get([D], F32) for _ in range(2)]; x1t_tr = [Tr(), Tr()]
    sqC = pc.get([D], BF16); sqC_tr = Tr()
    stC = [pc.get([8], F32) for _ in range(2)]; stC_tr = [Tr(), Tr()]
    xnC = [pc.get([D], BF16) for _ in range(2)]; xnC_tr = [Tr(), Tr()]
    scrC = (sqC, sqC_tr, stC, stC_tr, xnC, xnC_tr)
    x1_tr = Tr()
    oaT_v = oaT_d.rearrange("(k p) s -> p k s", p=128)

    def c_stage1(tb):
        obT, obT_tr, oaTb, oaTb_tr = obTs[tb % 2], obTs_tr[tb % 2], oaTbs[tb % 2], oaTbs_tr[tb % 2]
        for t_ in range(4):
            tile = tb * 4 + t_
            for g in range(3):
                dma(oBt[g].rearrange("p h c -> p (h c)"), oB_d[g][tile * 128:(tile + 1) * 128, :], reads=[oB_tr[g]],
                    writes=[oBt_tr[g]])
            tt("pool", osum, oBt[0], oBt[1], ALU.add, [oBt_tr[0], oBt_tr[1]], [osum_tr])
            tt("pool", osum, osum, oBt[2], ALU.add, [osum_tr, oBt_tr[2]], [osum_tr])
            P.op("dve", lambda e: e.reciprocal(out=rden, in_=osum[:, :, 64]), [osum_tr], [rden_tr])
            tt("dve", obn, osum[:, :, 0:64], rden.unsqueeze(2).broadcast_to([128, 8, 64]), ALU.mult, [osum_tr, rden_tr],
               [obn_tr])
            bk, bk_tr = bank()
            pt4 = bk[:, 0:256].bitcast(BF16).rearrange("p (a b) -> p a b", b=128)
            for hp in range(4):
                tp(pt4[:, hp, :], obn[:, 2 * hp:2 * hp + 2, :].rearrange("p h c -> p (h c)"), [obn_tr], [bk_tr])
            cp("act", obT[:, :, t_ * 128:(t_ + 1) * 128], pt4, [bk_tr], [obT_tr])
        dma(oaTb, oaT_v[:, :, tb * 512:(tb + 1) * 512], reads=[oaT_tr], writes=[oaTb_tr])

    pend_c = [None]
    c_stage1(0)
    for tb in range(8):
        if tb + 1 < 8:
            c_stage1(tb + 1)
        obT, obT_tr, oaTb, oaTb_tr = obTs[tb % 2], obTs_tr[tb % 2], oaTbs[tb % 2], oaTbs_tr[tb % 2]
        hts = hT_tr[tb * 4:tb * 4 + 4]
        for cbk in range(8):
            cs = slice(cbk * 128, (cbk + 1) * 128)
            bya, bya_tr = bank()
            byb, byb_tr = bank()
            bga, bga_tr = bank()
            bgb, bgb_tr = bank()
            for kc in range(4):
                mm(bya[:, :], Wdno[:, kc, cs], oaTb[:, kc, :], kc == 0, kc == 3, [Wc2_tr[3][cbk // 4], oaTb_tr], [bya_tr])
            for kc in range(4):
                mm(byb[:, :], Wato[:, kc, cs], obT[:, kc, :], kc == 0, kc == 3, [Wc2_tr[4][cbk // 4], obT_tr], [byb_tr])
            for kc in range(KC):
                mm(bga[:, :], Wga[:, kc, cs], hT[:, kc, 1 + tb * 512:1 + (tb + 1) * 512], kc == 0, kc == KC - 1,
                   [Wc2_tr[0][cbk // 4]] + hts, [bga_tr])
            for kc in range(KC):
                mm(bgb[:, :], Wgb[:, kc, cs], hT[:, kc, 1 + tb * 512:1 + (tb + 1) * 512], kc == 0, kc == KC - 1,
                   [Wc2_tr[1][cbk // 4]] + hts, [bgb_tr])
            act(sga, bga[:, :], AF.Sigmoid, [bga_tr], [sga_tr])
            act(sgb, bgb[:, :], AF.Sigmoid, [bgb_tr], [sgb_tr])
            tt("dve", sga, bya[:, :], sga, ALU.mult, [bya_tr, sga_tr], [sga_tr])
            tt("dve", sgb, byb[:, :], sgb, ALU.mult, [byb_tr, sgb_tr], [sgb_tr])
            tt("pool", mixT[:, cbk, :], sga, sgb, ALU.add, [sga_tr, sgb_tr], [mixT_tr])
        dma(xsC[(tb * 4) % 2], x_d[tb * 4 * 128:(tb * 4 + 1) * 128, :], writes=[xsC_tr[(tb * 4) % 2]], eng="pool")
        for t_ in range(4):
            tile = tb * 4 + t_
            i2 = tile % 2
            if t_ + 1 < 4:
                dma(xsC[1 - i2], x_d[(tile + 1) * 128:(tile + 2) * 128, :], writes=[xsC_tr[1 - i2]], eng="pool")
            for half in range(2):
                bk, bk_tr = bank()
                for cbk in range(8):
                    mm(bk[:, :], mixT[:, cbk, t_ * 128:(t_ + 1) * 128], Wo_[:, cbk, half * 512:(half + 1) * 512], cbk == 0, cbk == 7,
                       [mixT_tr, Wc2_tr[2][half]], [bk_tr])
                tt("dve", x1t[i2][:, half * 512:(half + 1) * 512], bk[:, :], xsC[i2][:, half * 512:(half + 1) * 512], ALU.add,
                   [bk_tr, xsC_tr[i2]], [x1t_tr[i2]])
            dma(x1_d[tile * 128:(tile + 1) * 128, :], x1t[i2], reads=[x1t_tr[i2]], writes=[x1_tr])
            nb_ = norm_tile(x1t[i2], x1t_tr[i2], tile, i2, 8, scrC, defer=True)
            if pend_c[0] is not None:
                pend_c[0]()
            pend_c[0] = nb_

    if pend_c[0] is not None:
        pend_c[0]()
        pend_c[0] = None
    if debug is not None and debug["what"] == "x1":
        barrier_all()
        for t_ in range(NT):
            dma(xsC[0], x1_d[t_ * 128:(t_ + 1) * 128, :], reads=[x1_tr], writes=[xsC_tr[0]])
            dma(dbg_d[t_ * 128:(t_ + 1) * 128, :], xsC[0], reads=[xsC_tr[0]])
        P.finish()
        P.emit()
        return nc
    barrier_all('C_end')

    pd = Alloc(PERS_END)
    PASSES = [list(range(0, 8)), list(range(8, 15)), list(range(15, 22))]
    NCH = 8
    wstD = [pd.get([1024], F32) for _ in range(3)]; wstD_tr = trs(3)
    wupS = [pd.get([KC, 2 * NCH, 128], BF16) for _ in range(2)]; wupS_tr = [Tr(), Tr()]
    wdnS = [pd.get([NCH, 1024], BF16) for _ in range(2)]; wdnS_tr = [Tr(), Tr()]
    fcw = pd.get([44, 3], F32); fcw_tr = Tr()
    cg = [pd.get([256], F32) for _ in range(2)]; cg_tr = [Tr(), Tr()]
    cu = [pd.get([256], F32) for _ in range(2)]; cu_tr = [Tr(), Tr()]
    ab_ = [pd.get([256], BF16) for _ in range(2)]; ab_tr_ = [Tr(), Tr()]
    xr = [pd.get([D], F32) for _ in range(2)]; xr_tr = [Tr(), Tr()]
    out_tr = Tr()
    dma(fcw, f_cw, writes=[fcw_tr])
    wrk_rr = [0]

    def bank_lo():
        i = wrk_rr[0] % 4
        wrk_rr[0] += 1
        return banks[i], bank_tr[i]

    wp_i = [0]

    def weight_pieces(ps):
        bs = ps % 2
        out = []
        for c, ch in enumerate(PASSES[ps]):
            for kind in range(3):
                def mk(c=c, ch=ch, kind=kind):
                    st_ = {}

                    def issue():
                        i3 = wp_i[0] % 3
                        wp_i[0] += 1
                        st_["i3"] = i3
                        if kind < 2:
                            col = kind * DFF + ch * 128
                            dma(wstD[i3], w_up[kind * 22 + ch], writes=[wstD_tr[i3]])
                        else:
                            dma(wstD[i3], w_dn[ch * 128:(ch + 1) * 128, :], writes=[wstD_tr[i3]])

                    def cast(eng):
                        i3 = st_["i3"]
                        if kind < 2:
                            sv = wstD[i3].rearrange("p (k c) -> p k c", c=128)
                            cp(eng, wupS[bs][:, :, kind * NCH + c, :], sv, [wstD_tr[i3]], [wupS_tr[bs]])
                        else:
                            cp(eng, wdnS[bs][:, c, :], wstD[i3], [wstD_tr[i3]], [wdnS_tr[bs]])
                    return issue, cast
                out.append(mk())
        return out

    pcs = weight_pieces(0)
    for i_, (iss, cst) in enumerate(pcs):
        iss()
        if i_ >= 2:
            pcs[i_ - 2][1]("act" if i_ % 2 else "dve")
    for i_ in range(len(pcs) - 2, len(pcs)):
        pcs[i_][1]("act" if i_ % 2 else "dve")

    ci = 0
    for ps in range(len(PASSES)):
        bs = ps % 2
        wup, wup_tr, wdn, wdn_tr = wupS[bs], wupS_tr[bs], wdnS[bs], wdnS_tr[bs]
        nch = len(PASSES[ps])
        nxt_pcs = weight_pieces(ps + 1) if ps + 1 < len(PASSES) else []
        pend_cast = []
        for tb in range(16):
            hts = [hT_tr[t_] for t_ in range(max(0, tb * 2 - 1), min(NT, tb * 2 + 3))]
            for cst in pend_cast:
                cst("act")
            pend_cast = []
            for _ in range(2):
                if nxt_pcs:
                    iss, cst = nxt_pcs.pop(0)
                    iss()
                    pend_cast.append(cst)
            accs = [(banks[4 + i], bank_tr[4 + i]) for i in range(4)]
            for t2 in range(2):
                tile = tb * 2 + t2
                if ps == 0:
                    dma(xr[t2], x1_d[tile * 128:(tile + 1) * 128, :], reads=[x1_tr], writes=[xr_tr[t2]])
                else:
                    dma(xr[t2], out_d[tile * 128:(tile + 1) * 128, :], reads=[out_tr], writes=[xr_tr[t2]])
            upb = {}

            def up(c):
                bkg, bkg_tr = bank_lo()
                bku, bku_tr = bank_lo()
                for kc in range(KC):
                    mm(bkg[:, 0:258], wup[:, kc, c, :], hT[:, kc, tb * 256:tb * 256 + 258], kc == 0, kc == KC - 1,
                       [wup_tr] + hts, [bkg_tr])
                for kc in range(KC):
                    mm(bku[:, 0:258], wup[:, kc, NCH + c, :], hT[:, kc, tb * 256:tb * 256 + 258], kc == 0, kc == KC - 1,
                       [wup_tr] + hts, [bku_tr])
                upb[c] = (bkg, bkg_tr, bku, bku_tr)

            up(0)
            for c in range(nch):
                ch = PASSES[ps][c]
                if c + 1 < nch:
                    up(c + 1)
                bkg, bkg_tr, bku, bku_tr = upb.pop(c)
                i2 = ci % 2
                ci += 1
                for (dst, dst_tr, src, src_tr, blk) in ((cg[i2], cg_tr[i2], bkg, bkg_tr, ch), (cu[i2], cu_tr[i2], bku, bku_tr, 22 + ch)):
                    act(dst, src[:, 1:257], AF.Copy, [src_tr, fcw_tr], [dst_tr], scale=fcw[:, blk, 1:2])
                    stt(dst, src[:, 0:256], fcw[:, blk, 0:1], dst, ALU.mult, ALU.add, [src_tr, fcw_tr, dst_tr], [dst_tr])
                    stt(dst, src[:, 2:258], fcw[:, blk, 2:3], dst, ALU.mult, ALU.add, [src_tr, fcw_tr, dst_tr], [dst_tr])
                act(cg[i2], cg[i2], AF.Silu, [cg_tr[i2]], [cg_tr[i2]])
                tt("pool", ab_[i2], cg[i2], cu[i2], ALU.mult, [cg_tr[i2], cu_tr[i2]], [ab_tr_[i2]])
                for t2 in range(2):
                    for half in range(2):
                        a_, a_tr = accs[t2 * 2 + half]
                        mm(a_[:, :], ab_[i2][:, t2 * 128:(t2 + 1) * 128], wdn[:, c, half * 512:(half + 1) * 512], c == 0, c == nch - 1,
                           [ab_tr_[i2], wdn_tr], [a_tr])
            for t2 in range(2):
                tile = tb * 2 + t2
                for half in range(2):
                    a_, a_tr = accs[t2 * 2 + half]
                    tt("dve", xr[t2][:, half * 512:(half + 1) * 512], a_[:, :], xr[t2][:, half * 512:(half + 1) * 512], ALU.add,
                       [a_tr, xr_tr[t2]], [xr_tr[t2]])
                dma(out_d[tile * 128:(tile + 1) * 128, :], xr[t2], reads=[xr_tr[t2]], writes=[out_tr])
        for cst in pend_cast:
            cst("act")
        while nxt_pcs:
            iss, cst = nxt_pcs.pop(0)
            iss()
            cst("act")
        mark('Dpass_end')

    P.finish()
    P.emit()
    return nc


def _host_inputs(inputs, b):
    f = np.float32
    w_in = np.ascontiguousarray(inputs["w_in"][0], dtype=f)
    a_cols = w_in[:, C_A:C_A + 16]
    b_cols = w_in[:, C_B:C_B + 16]
    w_ab = np.concatenate([a_cols[:, 0:8], b_cols[:, 0:8], a_cols[:, 8:16], b_cols[:, 8:16]], axis=1)
    m = {
        "x": np.ascontiguousarray(inputs["x"][b], dtype=f),
        "w_in": w_in,
        "w_ab": np.ascontiguousarray(w_ab),
        "norm1_w": np.ascontiguousarray(inputs["norm1_w"][0].reshape(KC, 128).T),
        "norm2_w": np.ascontiguousarray(inputs["norm2_w"][0].reshape(KC, 128).T),
        "dn_conv_w": np.ascontiguousarray(inputs["dn_conv_w"][0].reshape(3, 12, 128).transpose(2, 1, 0)),
        "dn_a_log": np.ascontiguousarray(inputs["dn_a_log"][0].reshape(1, 16)),
        "dn_dt_bias": np.ascontiguousarray(inputs["dn_dt_bias"][0].reshape(1, 16)),
        "dn_out_norm_w": np.ascontiguousarray(inputs["dn_out_norm_w"][0].reshape(1, 64)),
        "attn_q_norm_w": np.ascontiguousarray(np.tile(inputs["attn_q_norm_w"][0], 2).reshape(128, 1)),
        "attn_k_norm_w": np.ascontiguousarray(np.tile(inputs["attn_k_norm_w"][0], 2).reshape(128, 1)),
        "w_dn_out": np.ascontiguousarray(inputs["w_dn_out"][0]),
        "w_attn_out": np.ascontiguousarray(inputs["w_attn_out"][0]),
        "w_o": np.ascontiguousarray(inputs["w_o"][0]),
        "w_ffn_up": np.ascontiguousarray(
            np.asarray(inputs["w_ffn_up"][0], dtype=f).reshape(KC, 128, 44, 128).transpose(2, 1, 0, 3).reshape(44, 128, KC * 128)),
        "ffn_conv_w": np.ascontiguousarray(inputs["ffn_conv_w"][0].reshape(3, 44, 128).transpose(2, 1, 0)),
        "w_ffn_down": np.ascontiguousarray(inputs["w_ffn_down"][0]),
        "cb": _pack_bf16_consts(),
        "amask": _attn_masks(),
    }
    return m


def kernel(**inputs):
    nc = build()
    in_maps = [_host_inputs(inputs, b) for b in range(8)]
    res = run_bass_kernel_spmd(nc, in_maps, core_ids=list(range(8)))
    out = np.stack([np.asarray(res.results[b]["out"], dtype=np.float32) for b in range(8)], axis=0)
    return out
```
